# Optimizing a Trainium2 kernel written in Bass

```python
import math
import jax, jax.numpy as jnp
from jax import lax
import numpy as np

D_MODEL = 4096
BATCH = 4
SEQ = 2048
DEPTH = 1

HEAD_DIM = 128
N_A_HEADS = 16
N_A_KV_HEADS = 4
N_A_GROUP = N_A_HEADS // N_A_KV_HEADS
B_PATTERNS = ((128, 1), (512, 4), (2048, 16))
N_B_GROUPS = len(B_PATTERNS)
N_B_HEADS_PER_GROUP = 4
N_B_HEADS = N_B_GROUPS * N_B_HEADS_PER_GROUP
N_BRANCHES = 2
Q_BLOCK = 128
GRID_W = 64
ROPE_THETA = 10000.0
AXIS_ROPE_DIM = HEAD_DIM // 2
REL_BUCKETS = 32
REL_MAX_DIST = 1024
D_FF = ((8 * D_MODEL + 3 * 256 - 1) // (3 * 256)) * 256
EPS = 1e-6
NEG_INF = -1e30

A_Q_W = N_A_HEADS * HEAD_DIM
A_KV_W = N_A_KV_HEADS * HEAD_DIM
B_W = N_B_HEADS * HEAD_DIM
B_OUT_W = N_B_HEADS_PER_GROUP * HEAD_DIM
IN_W = A_Q_W + 2 * A_KV_W + 3 * B_W + N_BRANCHES * D_MODEL

kernel_name = "hybrid_gqa_axialrope_dilated_swa_griffin_merge"


def rms_norm(x, g):
    x32 = x.astype(jnp.float32)
    y = x32 * lax.rsqrt(jnp.mean(x32 * x32, axis=-1, keepdims=True) + EPS)
    return y.astype(x.dtype) * g


def rotate(xh, ang):
    half = xh.shape[-1] // 2
    x1, x2 = xh[..., :half], xh[..., half:]
    c = jnp.cos(ang)[None, :, None, :].astype(xh.dtype)
    s = jnp.sin(ang)[None, :, None, :].astype(xh.dtype)
    return jnp.concatenate([x1 * c - x2 * s, x2 * c + x1 * s], axis=-1)


def axial_rope(x, ang_row, ang_col):
    return jnp.concatenate([rotate(x[..., :AXIS_ROPE_DIM], ang_row),
                            rotate(x[..., AXIS_ROPE_DIM:], ang_col)], axis=-1)


def axial_angles(seq):
    rows = seq // GRID_W
    row = jnp.broadcast_to(jnp.arange(rows)[:, None], (rows, GRID_W)).reshape(-1).astype(jnp.float32)
    col = jnp.broadcast_to(jnp.arange(GRID_W)[None, :], (rows, GRID_W)).reshape(-1).astype(jnp.float32)
    inv = ROPE_THETA ** (-jnp.arange(0, AXIS_ROPE_DIM, 2, dtype=jnp.float32) / AXIS_ROPE_DIM)
    return row[:, None] * inv, col[:, None] * inv


def t5_bucket(rel):
    nb = REL_BUCKETS // 2
    max_exact = nb // 2
    side = jnp.where(rel > 0, nb, 0)
    n = jnp.abs(rel)
    nf = jnp.maximum(n, 1).astype(jnp.float32)
    large = max_exact + (jnp.log(nf / max_exact) / math.log(REL_MAX_DIST / max_exact)
                         * (nb - max_exact)).astype(jnp.int32)
    large = jnp.minimum(large, nb - 1)
    return side + jnp.where(n < max_exact, n, large)


def grid_attention(q, k, v):
    b, s = q.shape[:2]
    nb = s // Q_BLOCK
    qg = q.reshape(b, nb, Q_BLOCK, N_A_KV_HEADS, N_A_GROUP, HEAD_DIM).swapaxes(0, 1)
    k32 = k.astype(jnp.float32)
    scale = HEAD_DIM ** -0.5

    def one_block(qb):
        logits = jnp.einsum('bqkgd,bskd->bkgqs', qb.astype(jnp.float32), k32) * scale
        p = jax.nn.softmax(logits, axis=-1).astype(v.dtype)
        return jnp.einsum('bkgqs,bskd->bqkgd', p, v)

    o = lax.map(one_block, qg)
    return o.swapaxes(0, 1).reshape(b, s, A_Q_W)


def dilated_attention(q, k, v, rel_bias):
    b, s = q.shape[:2]
    nb = s // Q_BLOCK
    scale = HEAD_DIM ** -0.5
    offs, biases = [], []
    for g, (window, dil) in enumerate(B_PATTERNS):
        radius = window // (2 * dil)
        off = jnp.arange(-radius, radius + 1, dtype=jnp.int32) * dil
        offs.append(off)
        tab = rel_bias[t5_bucket(off)][:, g * N_B_HEADS_PER_GROUP:(g + 1) * N_B_HEADS_PER_GROUP]
        biases.append(tab.T.astype(jnp.float32))
    k_groups = [k[:, :, g].astype(jnp.float32) for g in range(N_B_GROUPS)]
    v_groups = [v[:, :, g] for g in range(N_B_GROUPS)]

    def one_block(i):
        t0 = i * Q_BLOCK
        tq = t0 + jnp.arange(Q_BLOCK, dtype=jnp.int32)
        qb = lax.dynamic_slice_in_dim(q, t0, Q_BLOCK, axis=1).astype(jnp.float32)
        outs, lses = [], []
        for g in range(N_B_GROUPS):
            idx = tq[:, None] + offs[g][None, :]
            valid = (idx >= 0) & (idx < s)
            idxc = jnp.clip(idx, 0, s - 1)
            kg = jnp.take(k_groups[g], idxc, axis=1)
            vg = jnp.take(v_groups[g], idxc, axis=1)
            logits = jnp.einsum('bqhd,bqjhd->bhqj', qb[:, :, g], kg) * scale + biases[g][None, :, None, :]
            logits = jnp.where(valid[None, None], logits, NEG_INF)
            lse = jax.nn.logsumexp(logits, axis=-1)
            p = jnp.exp(logits - lse[..., None]).astype(v.dtype)
            outs.append(jnp.einsum('bhqj,bqjhd->bqhd', p, vg))
            lses.append(lse)
        w = jax.nn.softmax(jnp.stack(lses, axis=0), axis=0)
        w = jnp.swapaxes(w, 2, 3).astype(v.dtype)
        return jnp.einsum('gbqh,gbqhd->bqhd', w, jnp.stack(outs, axis=0))

    o = lax.map(one_block, jnp.arange(nb, dtype=jnp.int32))
    return o.swapaxes(0, 1).reshape(b, s, B_OUT_W)


def setup_inputs(seed: int = 0) -> dict:
    key = jax.random.key(seed)
    ks = jax.random.split(key, 16)
    f32 = jnp.float32

    def nrm(k, shape, scale):
        return jax.random.normal(k, shape, f32) * scale

    return {
        "x": nrm(ks[0], (BATCH, SEQ, D_MODEL), 1.0),
        "norm1_g": 1.0 + nrm(ks[1], (DEPTH, D_MODEL), 0.02),
        "w_in": nrm(ks[2], (DEPTH, D_MODEL, IN_W), D_MODEL ** -0.5),
        "b_gate": nrm(ks[3], (DEPTH, N_BRANCHES, D_MODEL), 0.02),
        "q_norm_a": 1.0 + nrm(ks[4], (DEPTH, HEAD_DIM), 0.02),
        "k_norm_a": 1.0 + nrm(ks[5], (DEPTH, HEAD_DIM), 0.02),
        "q_norm_b": 1.0 + nrm(ks[6], (DEPTH, HEAD_DIM), 0.02),
        "k_norm_b": 1.0 + nrm(ks[7], (DEPTH, HEAD_DIM), 0.02),
        "rel_bias": nrm(ks[8], (REL_BUCKETS, N_B_HEADS), 0.2),
        "w_proj_a": nrm(ks[9], (DEPTH, A_Q_W, D_MODEL), A_Q_W ** -0.5),
        "w_proj_b": nrm(ks[10], (DEPTH, B_OUT_W, D_MODEL), B_OUT_W ** -0.5),
        "w_out": nrm(ks[11], (DEPTH, D_MODEL, D_MODEL), D_MODEL ** -0.5),
        "norm2_g": 1.0 + nrm(ks[12], (DEPTH, D_MODEL), 0.02),
        "w_ffn_gate": nrm(ks[13], (DEPTH, D_MODEL, D_FF), D_MODEL ** -0.5),
        "w_ffn_up": nrm(ks[14], (DEPTH, D_MODEL, D_FF), D_MODEL ** -0.5),
        "w_ffn_down": nrm(ks[15], (DEPTH, D_FF, D_MODEL), D_FF ** -0.5),
    }


def reference(x, norm1_g, w_in, b_gate, q_norm_a, k_norm_a, q_norm_b, k_norm_b, rel_bias,
              w_proj_a, w_proj_b, w_out, norm2_g, w_ffn_gate, w_ffn_up, w_ffn_down):
    b, s, _ = x.shape
    ang_row, ang_col = axial_angles(s)
    splits = [int(c) for c in np.cumsum([A_Q_W, A_KV_W, A_KV_W, B_W, B_W, B_W, D_MODEL])]
    b_shape = (b, s, N_B_GROUPS, N_B_HEADS_PER_GROUP, HEAD_DIM)
    for l in range(DEPTH):
        h = rms_norm(x, norm1_g[l])
        proj = h @ w_in[l]
        qa, ka, va, qb, kb, vb, ga, gb = jnp.split(proj, splits, axis=-1)
        qa = axial_rope(rms_norm(qa.reshape(b, s, N_A_HEADS, HEAD_DIM), q_norm_a[l]), ang_row, ang_col)
        ka = axial_rope(rms_norm(ka.reshape(b, s, N_A_KV_HEADS, HEAD_DIM), k_norm_a[l]), ang_row, ang_col)
        va = va.reshape(b, s, N_A_KV_HEADS, HEAD_DIM)
        o_a = grid_attention(qa, ka, va)
        qb = rms_norm(qb.reshape(b_shape), q_norm_b[l])
        kb = rms_norm(kb.reshape(b_shape), k_norm_b[l])
        vb = vb.reshape(b_shape)
        o_b = dilated_attention(qb, kb, vb, rel_bias)
        gate_a = jax.nn.sigmoid(ga + b_gate[l, 0])
        gate_b = jax.nn.sigmoid(gb + b_gate[l, 1])
        merged = gate_a * (o_a @ w_proj_a[l]) + gate_b * (o_b @ w_proj_b[l])
        x = x + merged @ w_out[l]
        h = rms_norm(x, norm2_g[l])
        x = x + (jax.nn.silu(h @ w_ffn_gate[l]) * (h @ w_ffn_up[l])) @ w_ffn_down[l]
    return x
```

```python
import math
from contextlib import ExitStack

import numpy as np
import concourse.bass as bass
import concourse.mybir as mybir
from concourse.bass_utils import run_bass_kernel_spmd

F32 = mybir.dt.float32
BF16 = mybir.dt.bfloat16
AF = mybir.ActivationFunctionType
ALU = mybir.AluOpType
AX = mybir.AxisListType

D = 4096
SEQ = 2048
BATCH = 4
HD = 128
DFF = 11008
T = 512
NS = 4
KC = D // 128
EPS = 1e-6
SCALE = HD ** -0.5
IN_W = 15872
NEG = -1e30
SW_OWN = 1920
SW_OTH = 1920
C_OWN = 896
C_OTH = 1920

NWSLOT = 6


class Op:
    __slots__ = ("eng", "fn", "deps", "signal", "val", "is_dma", "sem")


class Sched:
    ENGS = ("pe", "act", "dve", "pool", "sp")

    def __init__(self):
        self.ops = {e: [] for e in self.ENGS}
        self.state = {}
        self.dma_cnt = {}
        self.pending = {}
        self.uid = 0

    def _st(self, k):
        st = self.state.get(k)
        if st is None:
            st = {"w": None, "r": {}}
            self.state[k] = st
        return st

    def barrier(self, region):
        pend = []
        for k, st in self.state.items():
            if k[0] == region:
                if st["w"] is not None:
                    pend.append(st["w"])
                pend.extend(st["r"].values())
                st["w"] = None
                st["r"] = {}
        old = self.pending.get(region, [])
        self.pending[region] = list(set(pend + old))

    def add(self, eng, fn, reads=(), writes=(), dma=None):
        o = Op()
        o.eng = eng
        o.fn = fn
        o.is_dma = dma is not None
        o.signal = False
        o.val = 0
        o.sem = None
        deps = []
        for k in reads:
            st = self._st(k)
            if st["w"] is not None:
                deps.append(st["w"])
            if k[0] in self.pending:
                deps.extend(self.pending[k[0]])
        for k in writes:
            st = self._st(k)
            if st["w"] is not None:
                deps.append(st["w"])
            deps.extend(st["r"].values())
            if k[0] in self.pending:
                deps.extend(self.pending[k[0]])
        dd = []
        seen = set()
        for d in deps:
            if id(d) in seen:
                continue
            seen.add(id(d))
            if (not d.is_dma) and (not o.is_dma) and d.eng == "pe" and eng == "pe":
                continue
            d.signal = True
            dd.append(d)
        o.deps = dd
        if o.is_dma:
            n = self.dma_cnt.get(dma, 0) + 1
            self.dma_cnt[dma] = n
            o.sem = ("dma", dma)
            o.val = 16 * n
        self.uid += 1
        rkey = eng if not o.is_dma else ("dma", self.uid)
        for k in reads:
            self._st(k)["r"][rkey] = o
        for k in writes:
            st = self._st(k)
            st["w"] = o
            st["r"] = {}
        self.ops[eng].append(o)
        return o

    def emit(self, nc, es, final_waits):
        for e in self.ENGS:
            cnt = 0
            for o in self.ops[e]:
                if not o.is_dma and o.signal:
                    cnt += 1
                    o.val = cnt
                    o.sem = ("eng", e)
        sems = {}

        def sem_of(key):
            if key not in sems:
                nm = "s_" + "_".join(str(x) for x in key).replace(" ", "")
                sems[key] = es.enter_context(nc.semaphore(nm))
            return sems[key]

        for e in self.ENGS:
            for o in self.ops[e]:
                if o.sem is not None:
                    sem_of(o.sem)
        block = es.enter_context(nc.Block())

        def run(e):
            def body(engine):
                waited = {}
                for o in self.ops[e]:
                    for d in o.deps:
                        if waited.get(d.sem, 0) < d.val:
                            engine.wait_ge(sems[d.sem], d.val)
                            waited[d.sem] = d.val
                    ins = o.fn(engine)
                    if o.is_dma:
                        ins.then_inc(sems[o.sem], 16)
                    elif o.signal:
                        ins.then_inc(sems[o.sem], 1)
                if e == "sp":
                    for o in final_waits:
                        engine.wait_ge(sems[o.sem], o.val)
            return body

        block.tensor(run("pe"))
        block.scalar(run("act"))
        block.vector(run("dve"))
        block.gpsimd(run("pool"))
        block.sync(run("sp"))


class Builder:
    def __init__(self, debug=None):
        self.debug = debug or {}
        self.nc = bass.Bass("TRN2", target_bir_lowering=False)
        self.S = Sched()
        self.es = ExitStack()
        self.final = []
        self.wload_i = 0
        self.set_i = 0

    def setup(self):
        nc = self.nc
        dt = nc.dram_tensor
        self.x_seq = dt("x_seq", [SEQ, D], F32, kind="ExternalInput").ap()
        self.w_in = dt("w_in", [D, IN_W], F32, kind="ExternalInput").ap()
        self.w_pa = dt("w_proj_a", [2048, D], F32, kind="ExternalInput").ap()
        self.w_pb = dt("w_proj_b", [512, D], F32, kind="ExternalInput").ap()
        self.w_out = dt("w_out", [D, D], F32, kind="ExternalInput").ap()
        self.w_g = dt("w_ffn_gate", [D, DFF], F32, kind="ExternalInput").ap()
        self.w_u = dt("w_ffn_up", [D, DFF], F32, kind="ExternalInput").ap()
        self.w_d = dt("w_ffn_down", [DFF, D], F32, kind="ExternalInput").ap()
        self.c_small = dt("c_small", [128, 1024], F32, kind="ExternalInput").ap()
        self.c_rope = dt("c_rope", [128, 16 * 4 * 32], F32, kind="ExternalInput").ap()
        self.c_strip = dt("c_strip", [2, 12, 128, SW_OWN], F32, kind="ExternalInput").ap()
        self.out = dt("out", [1024, D], F32, kind="ExternalOutput").ap()
        self.kt_d = dt("kt_scratch", [16, 128, SEQ], BF16, kind="Internal").ap()
        self.v_d = dt("v_scratch", [SEQ, 2048], BF16, kind="Internal").ap()
        self.dbg = {}
        for name, shape in self.debug.items():
            self.dbg[name] = dt("dbg_" + name, list(shape), F32, kind="ExternalOutput").ap()

        ARENA = 52992
        self.arena = self.es.enter_context(nc.sbuf_tensor("arena", [128, ARENA], F32))
        self.psum = self.es.enter_context(nc.psum_tensor("psum", [128, 8, 512], F32))

    def view(self, byte_off, nbytes, dtype, pattern=None, **kw):
        assert byte_off % 4 == 0 and nbytes % 4 == 0
        v = self.arena[:, byte_off // 4:(byte_off + nbytes) // 4]
        if dtype == BF16:
            v = v.bitcast(BF16)
        if pattern:
            v = v.rearrange(pattern, **kw)
        return v

    def bank(self, i):
        return self.psum[:, i, :]

    def bank_bf(self, i):
        return self.psum[:, i, :].bitcast(BF16)

    def wslot_view(self, i):
        return self.view(i * 8192, 8192, BF16, "p (a b) -> p a b", b=512)

    def load_w(self, wd, k0, nk, c0, ncol):
        i = self.wload_i % NWSLOT
        self.wload_i += 1
        dst = self.wslot_view(i)[:, 0:nk, 0:ncol]
        src = wd[k0 * 128:(k0 + nk) * 128, c0:c0 + ncol].rearrange("(kc p) n -> p kc n", p=128)
        self.S.add("pool", lambda e, dst=dst, src=src: e.dma_start(out=dst, in_=src),
                   reads=(), writes=(("W", "slot", i),), dma=("w", i))
        return i

    def next_set(self):
        s = self.set_i % 2
        self.set_i += 1
        return [4 * s + j for j in range(4)]

    def stream_tok(self, wd, c0, ncol, nkc, act_fn, act_keys, kgran=8, krow0=0, subs=None):
        banks = self.next_set()
        k = 0
        while k < nkc:
            nk = min(kgran, nkc - k)
            slot = self.load_w(wd, krow0 + k, nk, c0, ncol)
            wv = self.wslot_view(slot)

            def fn(e, k=k, nk=nk, wv=wv):
                ins = None
                for s in (range(NS) if subs is None else subs):
                    for kk in range(nk):
                        ins = e.matmul(self.bank(banks[s])[:, 0:ncol],
                                       act_fn(k + kk)[:, s * 128:(s + 1) * 128],
                                       wv[:, kk, 0:ncol],
                                       start=(k + kk == 0), stop=(k + kk == nkc - 1))
                return ins
            self.S.add("pe", fn, reads=[("W", "slot", slot)] + list(act_keys),
                       writes=[("P", "bank", b) for b in banks])
            k += nk
        return banks

    def stream_feat(self, wd, krow0, c0, nkc, act_fn, act_keys, nj=4, kgran=8):
        banks = self.next_set()
        k = 0
        while k < nkc:
            nk = min(kgran, nkc - k)
            slot = self.load_w(wd, krow0 + k, nk, c0, nj * 128)
            wv = self.wslot_view(slot)

            def fn(e, k=k, nk=nk, wv=wv):
                ins = None
                for j in range(nj):
                    for kk in range(nk):
                        ins = e.matmul(self.bank(banks[j]),
                                       wv[:, kk, j * 128:(j + 1) * 128],
                                       act_fn(k + kk),
                                       start=(k + kk == 0), stop=(k + kk == nkc - 1))
                return ins
            self.S.add("pe", fn, reads=[("W", "slot", slot)] + list(act_keys),
                       writes=[("P", "bank", b) for b in banks[:nj]])
            k += nk
        return banks

    CB = NWSLOT * 8192

    def load_consts(self):
        S = self.S
        cb = self.CB
        self.c_sm = self.view(cb, 4096, F32)
        self.c_rp = self.view(cb + 4096, 8192, F32, "p (t a f) -> p t a f", a=4, f=32)
        self.identb = self.view(cb + 12288, 256, BF16)
        self.onesb = self.view(cb + 12544, 256, BF16)
        self.CEND = cb + 12800
        S.add("sp", lambda e: e.dma_start(out=self.c_sm, in_=self.c_small), writes=[("C", "sm")], dma=("c", 0))
        S.add("sp", lambda e: e.dma_start(out=self.view(cb + 4096, 8192, F32), in_=self.c_rope),
              writes=[("C", "rope")], dma=("c", 1))
        self.identf = self.c_sm[:, 0:128]
        self.g1T = self.c_sm[:, 128:160]
        self.g2T = self.c_sm[:, 160:192]
        self.bgT = self.c_sm[:, 192:256]
        self.qkg = self.c_sm[:, 256:768]
        S.add("dve", lambda e: e.tensor_copy(self.identb, self.identf), reads=[("C", "sm")], writes=[("C", "identb")])
        S.add("dve", lambda e: e.memset(self.onesb, 1.0), writes=[("C", "onesb")])

    DYN = 62464
    RA = DYN
    RB = DYN + 32768
    RX = DYN + 65536
    RT = DYN + 131072

    def rr_bank(self):
        b = self._rr % 8
        self._rr += 1
        return b
    _rr = 0

    def job(self, mm_fn, post_fn):
        banks = mm_fn()
        self.flush()
        self._pending = (post_fn, banks)

    _pending = None

    def flush(self):
        if self._pending is not None:
            fn, banks = self._pending
            self._pending = None
            fn(banks)

    def norm_T(self, srcs, gT, dstT, dst_key, xs_off, tag, subs=None, banks=None, phase="both"):
        S = self.S
        junk = self.view(self.RT, 8192, BF16)
        ssv = self.view(self.RT + 8192, 256, F32)
        for s in (range(NS) if subs is None else subs):
            xs = self.view(xs_off + (s % 2) * 16384, 16384, F32)
            xk = (xs_off_region(self, xs_off), "xs", tag, s % 2)
            src = srcs[s]
            ssk = ("RT", "ss", s)
            rsk = ("RT", "rs", s)
            ss = ssv[:, 4 * s:4 * s + 1]
            rs = ssv[:, 4 * s + 1:4 * s + 2]
            if phase in ("both", "front"):
                if src[0] == "dram":
                    S.add("sp", lambda e, xs=xs, a=src[1]: e.dma_start(out=xs, in_=a),
                          writes=[xk], dma=("xs", s % 2))
                    inp, inkeys = xs, [xk]
                else:
                    inp, inkeys = src[1], list(src[2])
                S.add("act", lambda e, inp=inp, ss=ss: e.activation(junk, inp, AF.Square, accum_out=ss),
                      reads=inkeys, writes=[ssk, ("RT", "junk")])
                S.add("act", lambda e, ss=ss, rs=rs: e.activation(rs, ss, AF.Ln, scale=1.0 / D, bias=EPS),
                      reads=[ssk], writes=[rsk])
                S.add("act", lambda e, rs=rs: e.activation(rs, rs, AF.Exp, scale=-0.5),
                      reads=[rsk], writes=[rsk])
                S.add("dve", lambda e, inp=inp, xs=xs, rs=rs: e.tensor_scalar_mul(xs, inp, rs),
                      reads=inkeys + [rsk], writes=[xk])
            if phase == "front":
                continue
            for q in range(KC // 4):
                b = self.rr_bank() if banks is None else banks[q % len(banks)]
                S.add("pe", lambda e, b=b, q=q, xs=xs: self._tr4(e, b, q, xs), reads=[xk, ("C", "sm")],
                      writes=[("P", "bank", b)])
                dst = dstT[:, 4 * q:4 * q + 4, s * 128:(s + 1) * 128]
                g = gT[:, 4 * q:4 * q + 4].unsqueeze(2).broadcast_to([128, 4, 128])
                pv = self.bank(b).rearrange("p (a t) -> p a t", t=128)
                S.add("dve", lambda e, dst=dst, pv=pv, g=g: e.tensor_tensor(dst, pv, g, ALU.mult),
                      reads=[("P", "bank", b), ("C", "sm")], writes=[dst_key])

    def _tr4(self, e, b, q, xs):
        ins = None
        for a in range(4):
            kc = 4 * q + a
            ins = e.transpose(self.bank(b)[:, a * 128:(a + 1) * 128], xs[:, kc * 128:(kc + 1) * 128], self.identf)
        return ins

    def post_qk(self, banks, tile_abs, gain_idx, rope, dest_fn, dest_key, subs=None):
        S = self.S
        RXb = self.RX
        gain = self.qkg[:, gain_idx * 128:(gain_idx + 1) * 128].unsqueeze(1).broadcast_to([128, 4, 128])
        subs = list(range(NS)) if subs is None else list(subs)
        for s in subs:
            b = banks[s]
            bk = ("P", "bank", b)
            sq = self.view(RXb + 32768, 2048, F32, "p (h d) -> p h d", d=128)
            yn = self.view(RXb + 34816, 2048, F32, "p (h d) -> p h d", d=128)
            ybf_off = RXb + 53248 + s * 1024
            ybf = self.view(ybf_off, 1024, BF16, "p (h d) -> p h d", d=128)
            ybk = ("RX", "ybf", s)
            qs = self.view(self.RT + 8448, 64, F32)
            qss = qs[:, 0:4]
            qrs = qs[:, 4:8]
            pv = self.bank(b).rearrange("p (h d) -> p h d", d=128)
            S.add("act", lambda e, sq=sq, pv=pv: e.activation(sq, pv, AF.Square), reads=[bk], writes=[("RX", "sq")])
            S.add("dve", lambda e, sq=sq, qss=qss: e.tensor_reduce(qss, sq, AX.X, ALU.add),
                  reads=[("RX", "sq")], writes=[("RT", "qss")])
            S.add("act", lambda e, qss=qss, qrs=qrs: e.activation(qrs, qss, AF.Ln, scale=1.0 / HD, bias=EPS),
                  reads=[("RT", "qss")], writes=[("RT", "qrs")])
            S.add("act", lambda e, qrs=qrs: e.activation(qrs, qrs, AF.Exp, scale=-0.5),
                  reads=[("RT", "qrs")], writes=[("RT", "qrs")])
            rb = qrs.unsqueeze(2).broadcast_to([128, 4, 128])
            S.add("dve", lambda e, yn=yn, pv=pv, rb=rb: e.tensor_tensor(yn, pv, rb, ALU.mult),
                  reads=[bk, ("RT", "qrs")], writes=[("RX", "yn")])
            if not rope:
                S.add("dve", lambda e, ybf=ybf, yn=yn: e.tensor_tensor(ybf, yn, gain, ALU.mult),
                      reads=[("RX", "yn"), ("C", "sm")], writes=[ybk])
            else:
                S.add("dve", lambda e, yn=yn: e.tensor_tensor(yn, yn, gain, ALU.mult),
                      reads=[("RX", "yn"), ("C", "sm")], writes=[("RX", "yn")])
                ts = tile_abs * NS + s
                cosb = self.c_rp[:, ts, 0:2, :].unsqueeze(1).broadcast_to([128, 4, 2, 32])
                sinb = self.c_rp[:, ts, 2:4, :].unsqueeze(1).broadcast_to([128, 4, 2, 32])
                y5 = self.view(RXb + 34816, 2048, F32, "p (h a t f) -> p h a t f", a=2, t=2, f=32)
                o5 = self.view(ybf_off, 1024, BF16, "p (h a t f) -> p h a t f", a=2, t=2, f=32)
                x1v = y5[:, :, :, 0, :]
                x2v = y5[:, :, :, 1, :]
                tt = [self.view(RXb + 36864 + i * 1024, 1024, F32, "p (h a f) -> p h a f", a=2, f=32)
                      for i in range(4)]
                rk = [("RX", "ropet", i) for i in range(4)]
                S.add("dve", lambda e, o=tt[0], a=x1v, c=cosb: e.tensor_tensor(o, a, c, ALU.mult),
                      reads=[("RX", "yn"), ("C", "rope")], writes=[rk[0]])
                S.add("dve", lambda e, o=tt[1], a=x2v, c=sinb: e.tensor_tensor(o, a, c, ALU.mult),
                      reads=[("RX", "yn"), ("C", "rope")], writes=[rk[1]])
                S.add("dve", lambda e, o=o5[:, :, :, 0, :], a=tt[0], c=tt[1]: e.tensor_tensor(o, a, c, ALU.subtract),
                      reads=[rk[0], rk[1]], writes=[ybk])
                S.add("dve", lambda e, o=tt[2], a=x2v, c=cosb: e.tensor_tensor(o, a, c, ALU.mult),
                      reads=[("RX", "yn"), ("C", "rope")], writes=[rk[2]])
                S.add("dve", lambda e, o=tt[3], a=x1v, c=sinb: e.tensor_tensor(o, a, c, ALU.mult),
                      reads=[("RX", "yn"), ("C", "rope")], writes=[rk[3]])
                S.add("dve", lambda e, o=o5[:, :, :, 1, :], a=tt[2], c=tt[3]: e.tensor_tensor(o, a, c, ALU.add),
                      reads=[rk[2], rk[3]], writes=[ybk])
        for s in subs:
            b = banks[s]
            bk = ("P", "bank", b)
            ybf_off = RXb + 53248 + s * 1024
            ybk = ("RX", "ybf", s)
            ybf2 = self.view(ybf_off, 1024, BF16)

            def trf(e, b=b, ybf2=ybf2):
                ins = None
                for h in range(4):
                    ins = e.transpose(self.bank_bf(b)[:, h * 128:(h + 1) * 128], ybf2[:, h * 128:(h + 1) * 128],
                                      self.identb)
                return ins
            S.add("pe", trf, reads=[ybk, ("C", "identb")], writes=[bk])
            dst = dest_fn(s)
            srcv = self.bank_bf(b)[:, 0:512].rearrange("p (h t) -> p h t", t=128)
            S.add("act", lambda e, dst=dst, srcv=srcv: e.copy(dst, srcv), reads=[bk], writes=[dest_key])

    def post_v(self, banks, tile, vcol0, subs=None):
        S = self.S
        for s in (range(NS) if subs is None else subs):
            b = banks[s]
            vst = self.view(self.RX + 51200 + (s % 2) * 1024, 1024, BF16)
            vk = ("RX", "vst", s % 2)
            S.add("act", lambda e, vst=vst, b=b: e.copy(vst, self.bank(b)), reads=[("P", "bank", b)], writes=[vk])
            r0 = tile * T + s * 128
            dst = self.v_d[r0:r0 + 128, vcol0:vcol0 + 512]
            S.add("sp", lambda e, dst=dst, vst=vst: e.dma_start(out=dst, in_=vst), reads=[vk],
                  writes=[("VD", vcol0 // 512)], dma=("vst", s % 2))

    CG_QA = [0, 1, 2, 3]
    CG_KA = 4
    CG_VA = 5
    CG_QB = [6, 7, 8]
    CG_KB = [9, 10, 11]
    CG_VB = [12, 13, 14]

    h1_off = None

    def h1T_view(self, off=None):
        off = self.h1_off if off is None else off
        return self.view(off, 32768, BF16, "p (k t) -> p k t", t=T)

    def h1_key(self, off=None):
        off = self.h1_off if off is None else off
        return ("RA" if off == self.RA else "RB", "h1T")

    def compute_h1T(self, tile, off=None, subs=None, banks=None, phase="both"):
        if off is None:
            off = self.RA
        srcs = [("dram", self.x_seq[tile * T + s * 128: tile * T + (s + 1) * 128, :]) for s in range(NS)]
        self.norm_T(srcs, self.g1T, self.h1T_view(off), self.h1_key(off), self.RX, "n1", subs=subs, banks=banks,
                    phase=phase)

    def kv_pass(self, tile, hooks=None):
        h1T = self.h1T_view()
        act = lambda kc: h1T[:, kc, :]
        akeys = [self.h1_key()]
        kcount = [0]

        def k_job(cg, head0, gain_idx, rope, subs=None):
            ki = kcount[0] % 2
            kcount[0] += 1
            kst = self.view(self.RX + 43008 + ki * 4096, 4096, BF16, "p (h t) -> p h t", t=T)
            kk = ("RX", "kst", ki)
            sl = list(range(NS)) if subs is None else list(subs)
            c_lo, c_hi = sl[0] * 128, (sl[-1] + 1) * 128

            def post(banks):
                self.post_qk(banks, tile, gain_idx, rope,
                             lambda s: kst[:, :, s * 128:(s + 1) * 128], kk, subs=sl)
                dst = self.kt_d[head0:head0 + 4, :, tile * T + c_lo:tile * T + c_hi].rearrange("h p t -> p h t")
                self.S.add("sp", lambda e: e.dma_start(out=dst, in_=kst[:, :, c_lo:c_hi]), reads=[kk],
                           writes=[("KD", head0 // 4)], dma=("kst", ki))
            self.job(lambda: self.stream_tok(self.w_in, cg * 512, 512, KC, act, akeys, subs=subs), post)

        def v_job(cg, vcol0, subs=None):
            self.job(lambda: self.stream_tok(self.w_in, cg * 512, 512, KC, act, akeys, subs=subs),
                     lambda banks: self.post_v(banks, tile, vcol0, subs=subs))

        gsubs = {0: None, 1: None, 2: None}
        if tile in self.B_SUBS:
            gsubs = {0: self.B_SUBS[tile][0], 1: self.B_SUBS[tile][1], 2: None}
        jobs = [lambda: k_job(self.CG_KA, 0, 1, True), lambda: v_job(self.CG_VA, 0)]
        for g in range(3):
            jobs.append(lambda g=g: k_job(self.CG_KB[g], 4 + 4 * g, 3, False, subs=gsubs[g]))
            jobs.append(lambda g=g: v_job(self.CG_VB[g], 512 * (g + 1), subs=gsubs[g]))
        for i, j in enumerate(jobs):
            j()
            if hooks and i in hooks:
                hooks[i]()

    B_SUBS = {2: {0: [0], 1: [0, 1]}, 3: {0: [3], 1: [2, 3]}}

    def free_set_banks(self):
        s_ = self.set_i % 2
        return [4 * s_ + j for j in range(4)]

    def q_pass(self, tile):
        h1T = self.h1T_view()
        act = lambda kc: h1T[:, kc, :]
        akeys = [self.h1_key()]
        QT = self.view(self.RB, 28672, BF16, "p (h t) -> p h t", t=T)
        allq = self.CG_QA + self.CG_QB
        for i in (0, 4, 1, 5, 2, 6, 3):
            cg = allq[i]
            head0 = 4 * i
            rope = i < 4
            gain_idx = 0 if rope else 2

            def post(banks, head0=head0, rope=rope, gain_idx=gain_idx):
                self.post_qk(banks, tile, gain_idx, rope,
                             lambda s: QT[:, head0:head0 + 4, s * 128:(s + 1) * 128], ("RB", "QT"))
            self.job(lambda cg=cg: self.stream_tok(self.w_in, cg * 512, 512, KC, act, akeys), post)

    def attention(self, tile):
        S = self.S
        self.flush()
        S.barrier("RX")
        S.barrier("RT")
        RX, RT = self.RX, self.RT
        QT = self.view(self.RB, 28672, BF16, "p (h t) -> p h t", t=T)
        oT = self.view(RX + 45056, 20480, BF16, "p (h t) -> p h t", t=T)
        kt = [self.view(RX + i * 4096, 4096, BF16) for i in range(2)]
        vv = [self.view(RX + 8192 + i * 4096, 4096, BF16, "p (c d) -> p c d", d=128) for i in range(2)]
        strips = [self.view(RX + 16384, 7680, F32), self.view(RX + 24064, 7680, F32)]
        accO = self.view(RX + 31744, 8192, F32, "p (s t) -> p s t", t=T)
        accD = self.view(RT, 8192, F32, "p (s t) -> p s t", t=T)
        NP_, NT_ = 4, 3
        P = [self.view(RT + 8192 + i * 1024, 1024, BF16) for i in range(NP_)]
        tmp = [self.view(RT + 12288 + i * 2048, 2048, F32) for i in range(NT_)]
        st = {"kv": 0, "p": 0, "t": 0, "s": 0, "od": 0}
        q0 = tile * T

        def load_kv(kvidx, vcol):
            sl = st["kv"] % 2
            st["kv"] += 1
            S.add("sp", lambda e: e.dma_start(out=kt[sl], in_=self.kt_d[kvidx]),
                  reads=[("KD", kvidx // 4)], writes=[("RX", "kt", sl)], dma=("kt", sl))
            src = self.v_d[:, vcol:vcol + 128].rearrange("(c p) d -> p c d", p=128)
            S.add("sp", lambda e: e.dma_start(out=vv[sl], in_=src),
                  reads=[("VD", vcol // 512)], writes=[("RX", "v", sl)], dma=("v", sl))
            return sl

        def load_strip(bh, w):
            g_ = bh // 4
            offs = [q0 - c * 128 + (C_OWN if w == 0 else C_OTH) for c in needed_chunks(g_) if (c >= 8) == (w == 1)]
            if not offs:
                return
            lo, hi = min(offs), max(offs) + T
            S.add("sp", lambda e: e.dma_start(out=strips[w][:, lo:hi], in_=self.c_strip[w, bh][:, lo:hi]),
                  writes=[("RX", "strip", w)], dma=("strip", w))

        items = []
        deferred = []

        def add_head(qidx, kvload, bh, epilogue, chunks):
            ctx = {"qidx": qidx, "kvload": kvload, "bh": bh, "epi": epilogue, "sl": None, "sb": {}}
            for i, c in enumerate(chunks):
                items.append((ctx, c, i == 0, i == len(chunks) - 1))

        _nc_cache = {}

        def needed_chunks(g):
            if g in _nc_cache:
                return _nc_cache[g]
            idx = _STRIP_IDX
            out = []
            for c in range(16):
                w = 0 if c < 8 else 1
                off = q0 - c * 128 + (C_OWN if w == 0 else C_OTH)
                need = False
                for hh in range(2):
                    if (idx[(w, hh, g)][:, off:off + T] != 32).any():
                        need = True
                if need:
                    out.append(c)
            if g in (0, 1):
                for c in out:
                    if c >= 8:
                        assert (c % 4) in self.B_SUBS[c // 4][g], (g, c)
            _nc_cache[g] = out
            return out

        for kv in range(4):
            for g in range(4):
                h = kv * 4 + g

                def epi(ob, db, h=h):
                    ti = st["t"] % NT_
                    st["t"] += 1
                    S.add("dve", lambda e: e.reciprocal(tmp[ti], self.bank(db)),
                          reads=[("P", "bank", db)], writes=[("RT", "tmp", ti)])
                    S.add("dve", lambda e: e.tensor_tensor(oT[:, h, :], self.bank(ob), tmp[ti], ALU.mult),
                          reads=[("P", "bank", ob), ("RT", "tmp", ti)], writes=[("RX", "oT")])
                add_head(h, (kv, kv * 128) if g == 0 else None, None, epi, list(range(16)))
        for g in range(3):
            chunks = needed_chunks(g)
            for s4 in range(4):
                bh = 4 * g + s4

                def epi(ob, db, g=g, s4=s4):
                    ak = ("RX", "acc", s4)
                    if g == 0:
                        S.add("dve", lambda e: e.tensor_copy(accO[:, s4, :], self.bank(ob)),
                              reads=[("P", "bank", ob)], writes=[ak])
                        S.add("dve", lambda e: e.tensor_copy(accD[:, s4, :], self.bank(db)),
                              reads=[("P", "bank", db)], writes=[("RT", "accD", s4)])
                    else:
                        S.add("dve", lambda e: e.tensor_tensor(accO[:, s4, :], self.bank(ob), accO[:, s4, :], ALU.add),
                              reads=[("P", "bank", ob), ak], writes=[ak])
                        S.add("dve", lambda e: e.tensor_tensor(accD[:, s4, :], self.bank(db), accD[:, s4, :], ALU.add),
                              reads=[("P", "bank", db), ("RT", "accD", s4)], writes=[("RT", "accD", s4)])
                    if g == 2:
                        def fin(s4=s4, ak=ak):
                            S.add("dve", lambda e: e.reciprocal(accD[:, s4, :], accD[:, s4, :]),
                                  reads=[("RT", "accD", s4)], writes=[("RT", "accD", s4)])
                            S.add("dve", lambda e: e.tensor_tensor(oT[:, 16 + s4, :], accO[:, s4, :], accD[:, s4, :], ALU.mult),
                                  reads=[ak, ("RT", "accD", s4)], writes=[("RX", "oT")])
                        deferred.append(fin)
                add_head(16 + bh, (4 + bh, 512 * (g + 1) + s4 * 128), bh, epi, chunks)

        loads = []
        for (ctx, c, first, last) in items:
            if first:
                if ctx["kvload"] is not None:
                    loads.append(ctx["kvload"])
                ctx["li"] = len(loads) - 1
                ctx["sl"] = ctx["li"] % 2
        issued = [0]

        def ensure_loads(upto):
            while issued[0] <= min(upto, len(loads) - 1):
                load_kv(*loads[issued[0]])
                issued[0] += 1

        def issue_S(i):
            ctx, c, first, last = items[i]
            if first:
                ensure_loads(ctx["li"])
            sl = ctx["sl"]
            qidx = ctx["qidx"]
            sb = st["s"] % 4
            st["s"] += 1
            ctx["sb"][c] = sb
            S.add("pe", lambda e: e.matmul(self.bank(sb), kt[sl][:, c * 128:(c + 1) * 128], QT[:, qidx, :],
                                           start=True, stop=True),
                  reads=[("RX", "kt", sl), ("RB", "QT")], writes=[("P", "bank", sb)])

        LOOK = 3
        for i in range(min(LOOK, len(items))):
            issue_S(i)
        for i in range(len(items)):
            if i + LOOK < len(items):
                issue_S(i + LOOK)
            ctx, c, first, last = items[i]
            if first:
                pair = st["od"] % 2
                st["od"] += 1
                ctx["ob"], ctx["db"] = 4 + 2 * pair, 5 + 2 * pair
                if ctx["kvload"] is not None:
                    ensure_loads(ctx["li"] + 1)
                if ctx["bh"] is not None:
                    if not st.get("s0_loaded", False):
                        load_strip(ctx["bh"], 0)
                    st["s0_loaded"] = False
                    load_strip(ctx["bh"], 1)
            ob, db, sl, bh = ctx["ob"], ctx["db"], ctx["sl"], ctx["bh"]
            sb = ctx["sb"][c]
            pi = st["p"] % NP_
            st["p"] += 1
            if bh is None:
                S.add("act", lambda e, sb=sb, pi=pi: e.activation(P[pi], self.bank(sb), AF.Exp, scale=SCALE),
                      reads=[("P", "bank", sb)], writes=[("RT", "P", pi)])
            else:
                ti = st["t"] % NT_
                st["t"] += 1
                w = 0 if c < 8 else 1
                off = q0 - c * 128 + (C_OWN if w == 0 else C_OTH)
                assert 0 <= off and off + T <= SW_OWN
                sv = strips[w][:, off:off + T]
                S.add("dve", lambda e, sb=sb, ti=ti, sv=sv: e.scalar_tensor_tensor(
                    tmp[ti], self.bank(sb), SCALE, sv, ALU.mult, ALU.add),
                    reads=[("P", "bank", sb), ("RX", "strip", w)], writes=[("RT", "tmp", ti)])
                S.add("act", lambda e, ti=ti, pi=pi: e.activation(P[pi], tmp[ti], AF.Exp),
                      reads=[("RT", "tmp", ti)], writes=[("RT", "P", pi)])

            def pv(e, c=c, pi=pi, ob=ob, db=db, sl=sl, first=first, last=last):
                e.matmul(self.bank(ob), vv[sl][:, c, :], P[pi], start=first, stop=last)
                return e.matmul(self.bank(db), self.onesb, P[pi], start=first, stop=last)
            S.add("pe", pv, reads=[("RX", "v", sl), ("RT", "P", pi), ("C", "onesb")],
                  writes=[("P", "bank", ob), ("P", "bank", db)])
            if bh is not None and c < 8:
                rest = [items[j][1] for j in range(i + 1, len(items)) if items[j][0] is ctx]
                if not any(cc < 8 for cc in rest):
                    nxt = [items[j][0] for j in range(i + 1, len(items)) if items[j][0] is not ctx]
                    if nxt and nxt[0]["bh"] is not None:
                        load_strip(nxt[0]["bh"], 0)
                        st["s0_loaded"] = True
            if last:
                ctx["epi"](ob, db)
        for fin in deferred:
            fin()

    def merge(self, tile):
        S = self.S
        self.flush()
        S.barrier("RB")
        S.barrier("RX")
        S.barrier("RT")
        RX = self.RX
        h1T = self.h1T_view()
        oT = self.view(RX + 45056, 20480, BF16, "p (h t) -> p h t", t=T)
        mergedT = self.view(self.RB, 32768, BF16, "p (k t) -> p k t", t=T)
        sa = self.view(RX, 8192, F32, "p (j t) -> p j t", t=T)
        sb_ = self.view(RX + 8192, 8192, F32, "p (j t) -> p j t", t=T)
        mm = self.view(RX + 16384, 8192, F32, "p (j t) -> p j t", t=T)
        tt = self.view(RX + 24576, 2048, F32)
        hk = [("RA", "h1T")]
        ok = [("RX", "oT")]
        for cg in range(8):
            def post_ga(banks, cg=cg):
                for j in range(4):
                    bias = self.bgT[:, cg * 4 + j:cg * 4 + j + 1]
                    S.add("act", lambda e, j=j, b=banks[j], bias=bias: e.activation(sa[:, j, :], self.bank(b), AF.Sigmoid, bias=bias),
                          reads=[("P", "bank", banks[j]), ("C", "sm")], writes=[("RX", "sa")])
            self.job(lambda cg=cg: self.stream_feat(self.w_in, 0, 7680 + cg * 512, KC, lambda kc: h1T[:, kc, :], hk), post_ga)

            def post_pa(banks):
                for j in range(4):
                    S.add("dve", lambda e, j=j, b=banks[j]: e.tensor_tensor(mm[:, j, :], self.bank(b), sa[:, j, :], ALU.mult),
                          reads=[("P", "bank", banks[j]), ("RX", "sa")], writes=[("RX", "mm")])
            self.job(lambda cg=cg: self.stream_feat(self.w_pa, 0, cg * 512, 16, lambda kc: oT[:, kc, :], ok), post_pa)

            def post_gb(banks, cg=cg):
                for j in range(4):
                    bias = self.bgT[:, 32 + cg * 4 + j:32 + cg * 4 + j + 1]
                    S.add("act", lambda e, j=j, b=banks[j], bias=bias: e.activation(sb_[:, j, :], self.bank(b), AF.Sigmoid, bias=bias),
                          reads=[("P", "bank", banks[j]), ("C", "sm")], writes=[("RX", "sb")])
            self.job(lambda cg=cg: self.stream_feat(self.w_in, 0, 11776 + cg * 512, KC, lambda kc: h1T[:, kc, :], hk), post_gb)

            def post_pb(banks, cg=cg):
                for j in range(4):
                    S.add("dve", lambda e, j=j, b=banks[j]: e.tensor_tensor(tt, self.bank(b), sb_[:, j, :], ALU.mult),
                          reads=[("P", "bank", banks[j]), ("RX", "sb")], writes=[("RX", "tt")])
                    S.add("dve", lambda e, j=j: e.tensor_tensor(mergedT[:, cg * 4 + j, :], tt, mm[:, j, :], ALU.add),
                          reads=[("RX", "tt"), ("RX", "mm")], writes=[("RB", "mergedT")])
            self.job(lambda cg=cg: self.stream_feat(self.w_pb, 0, cg * 512, 4, lambda kc: oT[:, 16 + kc, :], ok), post_pb)

    def wout(self, tile):
        S = self.S
        self.flush()
        S.barrier("RA")
        S.barrier("RX")
        mergedT = self.view(self.RB, 32768, BF16, "p (k t) -> p k t", t=T)
        x1 = self.view(self.RX, 65536, F32, "p (s d) -> p s d", d=D)
        xres = [self.view(self.RA + i * 8192, 8192, F32, "p (s c) -> p s c", c=512) for i in range(2)]
        for cg in range(8):
            i = cg % 2
            src = self.x_seq[tile * T:(tile + 1) * T, cg * 512:(cg + 1) * 512].rearrange("(s p) c -> p s c", p=128)
            S.add("sp", lambda e, i=i, src=src: e.dma_start(out=xres[i], in_=src), writes=[("RA", "xres", i)], dma=("xres", i))

            def post(banks, cg=cg, i=i):
                for s in range(NS):
                    S.add("dve", lambda e, s=s, b=banks[s]: e.tensor_tensor(x1[:, s, cg * 512:(cg + 1) * 512], self.bank(b), xres[i][:, s, :], ALU.add),
                          reads=[("P", "bank", banks[s]), ("RA", "xres", i)], writes=[("RX", "x1", s)])
            self.job(lambda cg=cg: self.stream_tok(self.w_out, cg * 512, 512, KC, lambda kc: mergedT[:, kc, :], [("RB", "mergedT")]), post)

    def ffn(self, tile):
        S = self.S
        self.flush()
        S.barrier("RA")
        S.barrier("RB")
        S.barrier("RT")
        x1 = self.view(self.RX, 65536, F32, "p (s d) -> p s d", d=D)
        h2T = self.view(self.RB, 32768, BF16, "p (k t) -> p k t", t=T)
        srcs = [("sbuf", x1[:, s, :], [("RX", "x1", s)]) for s in range(NS)]
        self.norm_T(srcs, self.g2T, h2T, ("RB", "h2T"), self.RA, "n2")
        S.barrier("RA")
        sg = [self.view(self.RA + i * 8192, 8192, F32, "p (j t) -> p j t", t=T) for i in range(2)]
        aT = [self.view(self.RA + 16384 + i * 4096, 4096, BF16, "p (j t) -> p j t", t=T) for i in range(2)]
        hk = [("RB", "h2T")]
        gsz = [4] * 20 + [3, 3]
        gst = [sum(gsz[:i]) for i in range(len(gsz))]
        NG = len(gsz)
        assert sum(gsz) * 128 == DFF

        def njof(i):
            return gsz[i]

        def GU(i):
            nj = njof(i)
            p = i % 2

            def post_g(banks):
                for j in range(nj):
                    S.add("act", lambda e, j=j, b=banks[j]: e.activation(sg[p][:, j, :], self.bank(b), AF.Silu),
                          reads=[("P", "bank", banks[j])], writes=[("RA", "sg", p)])
            self.job(lambda: self.stream_feat(self.w_g, 0, gst[i] * 128, KC, lambda kc: h2T[:, kc, :], hk, nj=nj), post_g)

            def post_u(banks):
                for j in range(nj):
                    S.add("dve", lambda e, j=j, b=banks[j]: e.tensor_tensor(aT[p][:, j, :], self.bank(b), sg[p][:, j, :], ALU.mult),
                          reads=[("P", "bank", banks[j]), ("RA", "sg", p)], writes=[("RA", "aT", p)])
            self.job(lambda: self.stream_feat(self.w_u, 0, gst[i] * 128, KC, lambda kc: h2T[:, kc, :], hk, nj=nj), post_u)

        def Dn(i):
            nj = njof(i)
            p = i % 2
            for cg in range(8):
                def post(banks, cg=cg):
                    for s in range(NS):
                        xs_ = x1[:, s, cg * 512:(cg + 1) * 512]
                        S.add("dve", lambda e, b=banks[s], xs_=xs_: e.tensor_tensor(xs_, self.bank(b), xs_, ALU.add),
                              reads=[("P", "bank", banks[s]), ("RX", "x1", s)], writes=[("RX", "x1", s)])
                    if i == NG - 1:
                        dst = self.out[tile * T:(tile + 1) * T, cg * 512:(cg + 1) * 512].rearrange("(s p) c -> p s c", p=128)
                        o = S.add("sp", lambda e: e.dma_start(out=dst, in_=x1[:, :, cg * 512:(cg + 1) * 512]),
                                  reads=[("RX", "x1", s_) for s_ in range(NS)], dma=("out", cg % 4))
                        self.final.append(o)
                self.job(lambda cg=cg: self.stream_tok(self.w_d, cg * 512, 512, nj, lambda kc: aT[p][:, kc, :],
                                                       [("RA", "aT", p)], krow0=gst[i]), post)
        GU(0)
        for i in range(NG):
            if i + 1 < NG:
                GU(i + 1)
            Dn(i)
        self.flush()

    def build(self):
        self.setup()
        self.load_consts()
        S = self.S
        order = (2, 3, 1, 0)
        bufs = {2: self.RB, 3: self.RA, 1: self.RB, 0: self.RA}
        self.compute_h1T(order[0], off=bufs[order[0]])
        for i, tile in enumerate(order):
            self.h1_off = bufs[tile]
            hooks = None
            if i + 1 < len(order):
                nt = order[i + 1]
                def h3(nt=nt):
                    self.compute_h1T(nt, off=bufs[nt], subs=(0, 1), banks=self.free_set_banks(), phase="back")
                    self.compute_h1T(nt, off=bufs[nt], subs=(2, 3), phase="front")
                hooks = {
                    1: lambda nt=nt: self.compute_h1T(nt, off=bufs[nt], subs=(0, 1), phase="front"),
                    3: h3,
                    5: lambda nt=nt: self.compute_h1T(nt, off=bufs[nt], subs=(2, 3), banks=self.free_set_banks(),
                                                      phase="back"),
                }
            self.kv_pass(tile, hooks)
        self.h1_off = self.RA
        S.barrier("RB")
        for tile in (0, 1):
            if tile == 1:
                self.flush()
                for r in ("RA", "RB", "RX", "RT"):
                    S.barrier(r)
                self.compute_h1T(1)
            self.q_pass(tile)
            self.attention(tile)
            self.merge(tile)
            self.wout(tile)
            self.ffn(tile)
        S.emit(self.nc, self.es, self.final)
        return self.nc


def xs_off_region(self, off):
    return "RX" if off >= self.RX and off < self.RT else ("RA" if off < self.RB else "RB")


def _t5_bucket_np(rel):
    nb = 16
    max_exact = 8
    rel = np.asarray(rel, dtype=np.int64)
    side = np.where(rel > 0, nb, 0)
    n = np.abs(rel)
    nf = np.maximum(n, 1).astype(np.float32)
    large = max_exact + (np.log(nf / np.float32(max_exact)) / np.float32(math.log(1024 / max_exact))
                         * np.float32(nb - max_exact)).astype(np.int32)
    large = np.minimum(large, nb - 1)
    return side + np.where(n < max_exact, n, large)


_B_PATTERNS = ((128, 1), (512, 4), (2048, 16))


def _strip_index_tables():
    out = {}
    kp = np.arange(128)[:, None]
    m = np.arange(SW_OWN)[None, :]
    for hh in range(2):
        for which in range(2):
            if which == 0:
                delta = kp - m + C_OWN
            else:
                delta = kp - m + C_OTH - hh * 2048
            for g, (window, dil) in enumerate(_B_PATTERNS):
                ok = (delta % dil == 0) & (np.abs(delta) <= (window // (2 * dil)) * dil)
                idx = np.where(ok, _t5_bucket_np(delta), 32)
                out[(which, hh, g)] = idx
    return out


_STRIP_IDX = _strip_index_tables()


def _rope_table(hh):
    pos = np.arange(SEQ)
    actual = np.where(pos < 1024, hh * 1024 + pos, (1 - hh) * 1024 + (pos - 1024))
    row = (actual // 64).astype(np.float32)
    col = (actual % 64).astype(np.float32)
    inv = (np.float32(10000.0) ** (-np.arange(0, 64, 2, dtype=np.float32) / np.float32(64))).astype(np.float32)
    ar = row[:, None] * inv[None, :]
    ac = col[:, None] * inv[None, :]
    tab = np.stack([np.cos(ar), np.cos(ac), np.sin(ar), np.sin(ac)], axis=1).astype(np.float32)
    tab = tab.reshape(16, 128, 4, 32).transpose(1, 0, 2, 3).reshape(128, 16 * 4 * 32)
    return np.ascontiguousarray(tab)


def prep_core_inputs(inputs, core):
    b, hh = core // 2, core % 2
    f32 = np.float32
    x = np.asarray(inputs["x"], dtype=f32)
    own = x[b, hh * 1024:(hh + 1) * 1024]
    oth = x[b, (1 - hh) * 1024:(2 - hh) * 1024]
    x_seq = np.ascontiguousarray(np.concatenate([own, oth], axis=0))
    cs = np.zeros((128, 1024), dtype=f32)
    cs[:, 0:128] = np.eye(128, dtype=f32)
    cs[:, 128:160] = np.asarray(inputs["norm1_g"], f32)[0].reshape(32, 128).T
    cs[:, 160:192] = np.asarray(inputs["norm2_g"], f32)[0].reshape(32, 128).T
    bg = np.asarray(inputs["b_gate"], f32)[0]
    cs[:, 192:224] = bg[0].reshape(32, 128).T
    cs[:, 224:256] = bg[1].reshape(32, 128).T
    for i, nm in enumerate(["q_norm_a", "k_norm_a", "q_norm_b", "k_norm_b"]):
        cs[:, 256 + i * 128:256 + (i + 1) * 128] = np.asarray(inputs[nm], f32)[0][None, :]
    rb = np.asarray(inputs["rel_bias"], f32)
    rb_ext = np.concatenate([rb, np.full((1, rb.shape[1]), NEG, dtype=f32)], axis=0)
    idx = _strip_index_tables()
    strips = np.empty((2, 12, 128, SW_OWN), dtype=f32)
    for which in range(2):
        for g in range(3):
            ii = idx[(which, hh, g)]
            for s in range(4):
                strips[which, 4 * g + s] = rb_ext[ii, 4 * g + s]
    m = {
        "x_seq": x_seq,
        "w_in": np.asarray(inputs["w_in"], f32)[0],
        "w_proj_a": np.asarray(inputs["w_proj_a"], f32)[0],
        "w_proj_b": np.asarray(inputs["w_proj_b"], f32)[0],
        "w_out": np.asarray(inputs["w_out"], f32)[0],
        "w_ffn_gate": np.asarray(inputs["w_ffn_gate"], f32)[0],
        "w_ffn_up": np.asarray(inputs["w_ffn_up"], f32)[0],
        "w_ffn_down": np.asarray(inputs["w_ffn_down"], f32)[0],
        "c_small": cs,
        "c_rope": _rope_table(hh),
        "c_strip": strips,
    }
    return m


_CACHE = {}


def kernel(**inputs):
    if "nc" not in _CACHE:
        _CACHE["nc"] = Builder().build()
    nc = _CACHE["nc"]
    in_maps = [prep_core_inputs(inputs, c) for c in range(8)]
    res = run_bass_kernel_spmd(nc, in_maps, core_ids=list(range(8)))
    out = np.empty((BATCH, SEQ, D), dtype=np.float32)
    for c in range(8):
        b, hh = c // 2, c % 2
        out[b, hh * 1024:(hh + 1) * 1024] = np.asarray(res.results[c]["out"], dtype=np.float32)
    return out
```

```python
import math
from contextlib import ExitStack

import numpy as np
import concourse.bass as bass
import concourse.mybir as mybir
from concourse.bass_utils import run_bass_kernel_spmd

F32 = mybir.dt.float32
BF16 = mybir.dt.bfloat16
AF = mybir.ActivationFunctionType
ALU = mybir.AluOpType
AX = mybir.AxisListType

D = 4096
SEQ = 2048
BATCH = 4
HD = 128
DFF = 11008
T = 512
NS = 4
KC = D // 128
EPS = 1e-6
SCALE = HD ** -0.5
IN_W = 15872
NEG = -1e30
SW_OWN = 1920
SW_OTH = 1920
C_OWN = 896
C_OTH = 1920

NWSLOT = 6


class Op:
    __slots__ = ("eng", "fn", "deps", "signal", "val", "is_dma", "sem")


class Sched:
    ENGS = ("pe", "act", "dve", "pool", "sp")

    def __init__(self):
        self.ops = {e: [] for e in self.ENGS}
        self.state = {}
        self.dma_cnt = {}
        self.pending = {}
        self.uid = 0

    def _st(self, k):
        st = self.state.get(k)
        if st is None:
            st = {"w": None, "r": {}}
            self.state[k] = st
        return st

    def barrier(self, region):
        pend = []
        for k, st in self.state.items():
            if k[0] == region:
                if st["w"] is not None:
                    pend.append(st["w"])
                pend.extend(st["r"].values())
                st["w"] = None
                st["r"] = {}
        old = self.pending.get(region, [])
        self.pending[region] = list(set(pend + old))

    def add(self, eng, fn, reads=(), writes=(), dma=None):
        o = Op()
        o.eng = eng
        o.fn = fn
        o.is_dma = dma is not None
        o.signal = False
        o.val = 0
        o.sem = None
        deps = []
        for k in reads:
            st = self._st(k)
            if st["w"] is not None:
                deps.append(st["w"])
            if k[0] in self.pending:
                deps.extend(self.pending[k[0]])
        for k in writes:
            st = self._st(k)
            if st["w"] is not None:
                deps.append(st["w"])
            deps.extend(st["r"].values())
            if k[0] in self.pending:
                deps.extend(self.pending[k[0]])
        dd = []
        seen = set()
        for d in deps:
            if id(d) in seen:
                continue
            seen.add(id(d))
            if (not d.is_dma) and (not o.is_dma) and d.eng == "pe" and eng == "pe":
                continue
            d.signal = True
            dd.append(d)
        o.deps = dd
        if o.is_dma:
            n = self.dma_cnt.get(dma, 0) + 1
            self.dma_cnt[dma] = n
            o.sem = ("dma", dma)
            o.val = 16 * n
        self.uid += 1
        rkey = eng if not o.is_dma else ("dma", self.uid)
        for k in reads:
            self._st(k)["r"][rkey] = o
        for k in writes:
            st = self._st(k)
            st["w"] = o
            st["r"] = {}
        self.ops[eng].append(o)
        return o

    def emit(self, nc, es, final_waits):
        for e in self.ENGS:
            cnt = 0
            for o in self.ops[e]:
                if not o.is_dma and o.signal:
                    cnt += 1
                    o.val = cnt
                    o.sem = ("eng", e)
        sems = {}

        def sem_of(key):
            if key not in sems:
                nm = "s_" + "_".join(str(x) for x in key).replace(" ", "")
                sems[key] = es.enter_context(nc.semaphore(nm))
            return sems[key]

        for e in self.ENGS:
            for o in self.ops[e]:
                if o.sem is not None:
                    sem_of(o.sem)
        block = es.enter_context(nc.Block())

        def run(e):
            def body(engine):
                waited = {}
                for o in self.ops[e]:
                    for d in o.deps:
                        if waited.get(d.sem, 0) < d.val:
                            engine.wait_ge(sems[d.sem], d.val)
                            waited[d.sem] = d.val
                    ins = o.fn(engine)
                    if o.is_dma:
                        ins.then_inc(sems[o.sem], 16)
                    elif o.signal:
                        ins.then_inc(sems[o.sem], 1)
                if e == "sp":
                    for o in final_waits:
                        engine.wait_ge(sems[o.sem], o.val)
            return body

        block.tensor(run("pe"))
        block.scalar(run("act"))
        block.vector(run("dve"))
        block.gpsimd(run("pool"))
        block.sync(run("sp"))


class Builder:
    def __init__(self, debug=None):
        self.debug = debug or {}
        self.nc = bass.Bass("TRN2", target_bir_lowering=False)
        self.S = Sched()
        self.es = ExitStack()
        self.final = []
        self.wload_i = 0
        self.set_i = 0

    def setup(self):
        nc = self.nc
        dt = nc.dram_tensor
        self.x_seq = dt("x_seq", [SEQ, D], F32, kind="ExternalInput").ap()
        self.w_in = dt("w_in", [D, IN_W], F32, kind="ExternalInput").ap()
        self.w_pa = dt("w_proj_a", [2048, D], F32, kind="ExternalInput").ap()
        self.w_pb = dt("w_proj_b", [512, D], F32, kind="ExternalInput").ap()
        self.w_out = dt("w_out", [D, D], F32, kind="ExternalInput").ap()
        self.w_g = dt("w_ffn_gate", [D, DFF], F32, kind="ExternalInput").ap()
        self.w_u = dt("w_ffn_up", [D, DFF], F32, kind="ExternalInput").ap()
        self.w_d = dt("w_ffn_down", [DFF, D], F32, kind="ExternalInput").ap()
        self.c_small = dt("c_small", [128, 1024], F32, kind="ExternalInput").ap()
        self.c_rope = dt("c_rope", [128, 16 * 4 * 32], F32, kind="ExternalInput").ap()
        self.c_strip = dt("c_strip", [2, 12, 128, SW_OWN], F32, kind="ExternalInput").ap()
        self.out = dt("out", [1024, D], F32, kind="ExternalOutput").ap()
        self.kt_d = dt("kt_scratch", [16, 128, SEQ], BF16, kind="Internal").ap()
        self.v_d = dt("v_scratch", [SEQ, 2048], BF16, kind="Internal").ap()
        self.dbg = {}
        for name, shape in self.debug.items():
            self.dbg[name] = dt("dbg_" + name, list(shape), F32, kind="ExternalOutput").ap()

        ARENA = 52992
        self.arena = self.es.enter_context(nc.sbuf_tensor("arena", [128, ARENA], F32))
        self.psum = self.es.enter_context(nc.psum_tensor("psum", [128, 8, 512], F32))

    def view(self, byte_off, nbytes, dtype, pattern=None, **kw):
        assert byte_off % 4 == 0 and nbytes % 4 == 0
        v = self.arena[:, byte_off // 4:(byte_off + nbytes) // 4]
        if dtype == BF16:
            v = v.bitcast(BF16)
        if pattern:
            v = v.rearrange(pattern, **kw)
        return v

    def bank(self, i):
        return self.psum[:, i, :]

    def bank_bf(self, i):
        return self.psum[:, i, :].bitcast(BF16)

    def wslot_view(self, i):
        return self.view(i * 8192, 8192, BF16, "p (a b) -> p a b", b=512)

    def load_w(self, wd, k0, nk, c0, ncol):
        i = self.wload_i % NWSLOT
        self.wload_i += 1
        dst = self.wslot_view(i)[:, 0:nk, 0:ncol]
        src = wd[k0 * 128:(k0 + nk) * 128, c0:c0 + ncol].rearrange("(kc p) n -> p kc n", p=128)
        self.S.add("pool", lambda e, dst=dst, src=src: e.dma_start(out=dst, in_=src),
                   reads=(), writes=(("W", "slot", i),), dma=("w", i))
        return i

    def next_set(self):
        s = self.set_i % 2
        self.set_i += 1
        return [4 * s + j for j in range(4)]

    def stream_tok(self, wd, c0, ncol, nkc, act_fn, act_keys, kgran=8, krow0=0, subs=None):
        banks = self.next_set()
        k = 0
        while k < nkc:
            nk = min(kgran, nkc - k)
            slot = self.load_w(wd, krow0 + k, nk, c0, ncol)
            wv = self.wslot_view(slot)

            def fn(e, k=k, nk=nk, wv=wv):
                ins = None
                for s in (range(NS) if subs is None else subs):
                    for kk in range(nk):
                        ins = e.matmul(self.bank(banks[s])[:, 0:ncol],
                                       act_fn(k + kk)[:, s * 128:(s + 1) * 128],
                                       wv[:, kk, 0:ncol],
                                       start=(k + kk == 0), stop=(k + kk == nkc - 1))
                return ins
            self.S.add("pe", fn, reads=[("W", "slot", slot)] + list(act_keys),
                       writes=[("P", "bank", b) for b in banks])
            k += nk
        return banks

    def stream_feat(self, wd, krow0, c0, nkc, act_fn, act_keys, nj=4, kgran=8):
        banks = self.next_set()
        k = 0
        while k < nkc:
            nk = min(kgran, nkc - k)
            slot = self.load_w(wd, krow0 + k, nk, c0, nj * 128)
            wv = self.wslot_view(slot)

            def fn(e, k=k, nk=nk, wv=wv):
                ins = None
                for j in range(nj):
                    for kk in range(nk):
                        ins = e.matmul(self.bank(banks[j]),
                                       wv[:, kk, j * 128:(j + 1) * 128],
                                       act_fn(k + kk),
                                       start=(k + kk == 0), stop=(k + kk == nkc - 1))
                return ins
            self.S.add("pe", fn, reads=[("W", "slot", slot)] + list(act_keys),
                       writes=[("P", "bank", b) for b in banks[:nj]])
            k += nk
        return banks

    CB = NWSLOT * 8192

    def load_consts(self):
        S = self.S
        cb = self.CB
        self.c_sm = self.view(cb, 4096, F32)
        self.c_rp = self.view(cb + 4096, 8192, F32, "p (t a f) -> p t a f", a=4, f=32)
        self.identb = self.view(cb + 12288, 256, BF16)
        self.onesb = self.view(cb + 12544, 256, BF16)
        self.CEND = cb + 12800
        S.add("sp", lambda e: e.dma_start(out=self.c_sm, in_=self.c_small), writes=[("C", "sm")], dma=("c", 0))
        S.add("sp", lambda e: e.dma_start(out=self.view(cb + 4096, 8192, F32), in_=self.c_rope),
              writes=[("C", "rope")], dma=("c", 1))
        self.identf = self.c_sm[:, 0:128]
        self.g1T = self.c_sm[:, 128:160]
        self.g2T = self.c_sm[:, 160:192]
        self.bgT = self.c_sm[:, 192:256]
        self.qkg = self.c_sm[:, 256:768]
        S.add("dve", lambda e: e.tensor_copy(self.identb, self.identf), reads=[("C", "sm")], writes=[("C", "identb")])
        S.add("dve", lambda e: e.memset(self.onesb, 1.0), writes=[("C", "onesb")])

    DYN = 62464
    RA = DYN
    RB = DYN + 32768
    RX = DYN + 65536
    RT = DYN + 131072

    def rr_bank(self):
        b = self._rr % 8
        self._rr += 1
        return b
    _rr = 0

    def job(self, mm_fn, post_fn):
        banks = mm_fn()
        self.flush()
        self._pending = (post_fn, banks)

    _pending = None

    def flush(self):
        if self._pending is not None:
            fn, banks = self._pending
            self._pending = None
            fn(banks)

    def norm_T(self, srcs, gT, dstT, dst_key, xs_off, tag, subs=None, banks=None, phase="both"):
        S = self.S
        junk = self.view(self.RT, 8192, BF16)
        ssv = self.view(self.RT + 8192, 256, F32)
        for s in (range(NS) if subs is None else subs):
            xs = self.view(xs_off + (s % 2) * 16384, 16384, F32)
            xk = (xs_off_region(self, xs_off), "xs", tag, s % 2)
            src = srcs[s]
            ssk = ("RT", "ss", s)
            rsk = ("RT", "rs", s)
            ss = ssv[:, 4 * s:4 * s + 1]
            rs = ssv[:, 4 * s + 1:4 * s + 2]
            if phase in ("both", "front"):
                if src[0] == "dram":
                    S.add("sp", lambda e, xs=xs, a=src[1]: e.dma_start(out=xs, in_=a),
                          writes=[xk], dma=("xs", s % 2))
                    inp, inkeys = xs, [xk]
                else:
                    inp, inkeys = src[1], list(src[2])
                S.add("act", lambda e, inp=inp, ss=ss: e.activation(junk, inp, AF.Square, accum_out=ss),
                      reads=inkeys, writes=[ssk, ("RT", "junk")])
                S.add("act", lambda e, ss=ss, rs=rs: e.activation(rs, ss, AF.Ln, scale=1.0 / D, bias=EPS),
                      reads=[ssk], writes=[rsk])
                S.add("act", lambda e, rs=rs: e.activation(rs, rs, AF.Exp, scale=-0.5),
                      reads=[rsk], writes=[rsk])
                S.add("dve", lambda e, inp=inp, xs=xs, rs=rs: e.tensor_scalar_mul(xs, inp, rs),
                      reads=inkeys + [rsk], writes=[xk])
            if phase == "front":
                continue
            for q in range(KC // 4):
                b = self.rr_bank() if banks is None else banks[q % len(banks)]
                S.add("pe", lambda e, b=b, q=q, xs=xs: self._tr4(e, b, q, xs), reads=[xk, ("C", "sm")],
                      writes=[("P", "bank", b)])
                dst = dstT[:, 4 * q:4 * q + 4, s * 128:(s + 1) * 128]
                g = gT[:, 4 * q:4 * q + 4].unsqueeze(2).broadcast_to([128, 4, 128])
                pv = self.bank(b).rearrange("p (a t) -> p a t", t=128)
                S.add("dve", lambda e, dst=dst, pv=pv, g=g: e.tensor_tensor(dst, pv, g, ALU.mult),
                      reads=[("P", "bank", b), ("C", "sm")], writes=[dst_key])

    def _tr4(self, e, b, q, xs):
        ins = None
        for a in range(4):
            kc = 4 * q + a
            ins = e.transpose(self.bank(b)[:, a * 128:(a + 1) * 128], xs[:, kc * 128:(kc + 1) * 128], self.identf)
        return ins

    def post_qk(self, banks, tile_abs, gain_idx, rope, dest_fn, dest_key, subs=None):
        S = self.S
        RXb = self.RX
        gain = self.qkg[:, gain_idx * 128:(gain_idx + 1) * 128].unsqueeze(1).broadcast_to([128, 4, 128])
        subs = list(range(NS)) if subs is None else list(subs)
        for s in subs:
            b = banks[s]
            bk = ("P", "bank", b)
            sq = self.view(RXb + 32768, 2048, F32, "p (h d) -> p h d", d=128)
            yn = self.view(RXb + 34816, 2048, F32, "p (h d) -> p h d", d=128)
            ybf_off = RXb + 53248 + s * 1024
            ybf = self.view(ybf_off, 1024, BF16, "p (h d) -> p h d", d=128)
            ybk = ("RX", "ybf", s)
            qs = self.view(self.RT + 8448, 64, F32)
            qss = qs[:, 0:4]
            qrs = qs[:, 4:8]
            pv = self.bank(b).rearrange("p (h d) -> p h d", d=128)
            S.add("act", lambda e, sq=sq, pv=pv: e.activation(sq, pv, AF.Square), reads=[bk], writes=[("RX", "sq")])
            S.add("dve", lambda e, sq=sq, qss=qss: e.tensor_reduce(qss, sq, AX.X, ALU.add),
                  reads=[("RX", "sq")], writes=[("RT", "qss")])
            S.add("act", lambda e, qss=qss, qrs=qrs: e.activation(qrs, qss, AF.Ln, scale=1.0 / HD, bias=EPS),
                  reads=[("RT", "qss")], writes=[("RT", "qrs")])
            S.add("act", lambda e, qrs=qrs: e.activation(qrs, qrs, AF.Exp, scale=-0.5),
                  reads=[("RT", "qrs")], writes=[("RT", "qrs")])
            rb = qrs.unsqueeze(2).broadcast_to([128, 4, 128])
            S.add("dve", lambda e, yn=yn, pv=pv, rb=rb: e.tensor_tensor(yn, pv, rb, ALU.mult),
                  reads=[bk, ("RT", "qrs")], writes=[("RX", "yn")])
            if not rope:
                S.add("dve", lambda e, ybf=ybf, yn=yn: e.tensor_tensor(ybf, yn, gain, ALU.mult),
                      reads=[("RX", "yn"), ("C", "sm")], writes=[ybk])
            else:
                S.add("dve", lambda e, yn=yn: e.tensor_tensor(yn, yn, gain, ALU.mult),
                      reads=[("RX", "yn"), ("C", "sm")], writes=[("RX", "yn")])
                ts = tile_abs * NS + s
                cosb = self.c_rp[:, ts, 0:2, :].unsqueeze(1).broadcast_to([128, 4, 2, 32])
                sinb = self.c_rp[:, ts, 2:4, :].unsqueeze(1).broadcast_to([128, 4, 2, 32])
                y5 = self.view(RXb + 34816, 2048, F32, "p (h a t f) -> p h a t f", a=2, t=2, f=32)
                o5 = self.view(ybf_off, 1024, BF16, "p (h a t f) -> p h a t f", a=2, t=2, f=32)
                x1v = y5[:, :, :, 0, :]
                x2v = y5[:, :, :, 1, :]
                tt = [self.view(RXb + 36864 + i * 1024, 1024, F32, "p (h a f) -> p h a f", a=2, f=32)
                      for i in range(4)]
                rk = [("RX", "ropet", i) for i in range(4)]
                S.add("dve", lambda e, o=tt[0], a=x1v, c=cosb: e.tensor_tensor(o, a, c, ALU.mult),
                      reads=[("RX", "yn"), ("C", "rope")], writes=[rk[0]])
                S.add("dve", lambda e, o=tt[1], a=x2v, c=sinb: e.tensor_tensor(o, a, c, ALU.mult),
                      reads=[("RX", "yn"), ("C", "rope")], writes=[rk[1]])
                S.add("dve", lambda e, o=o5[:, :, :, 0, :], a=tt[0], c=tt[1]: e.tensor_tensor(o, a, c, ALU.subtract),
                      reads=[rk[0], rk[1]], writes=[ybk])
                S.add("dve", lambda e, o=tt[2], a=x2v, c=cosb: e.tensor_tensor(o, a, c, ALU.mult),
                      reads=[("RX", "yn"), ("C", "rope")], writes=[rk[2]])
                S.add("dve", lambda e, o=tt[3], a=x1v, c=sinb: e.tensor_tensor(o, a, c, ALU.mult),
                      reads=[("RX", "yn"), ("C", "rope")], writes=[rk[3]])
                S.add("dve", lambda e, o=o5[:, :, :, 1, :], a=tt[2], c=tt[3]: e.tensor_tensor(o, a, c, ALU.add),
                      reads=[rk[2], rk[3]], writes=[ybk])
        for s in subs:
            b = banks[s]
            bk = ("P", "bank", b)
            ybf_off = RXb + 53248 + s * 1024
            ybk = ("RX", "ybf", s)
            ybf2 = self.view(ybf_off, 1024, BF16)

            def trf(e, b=b, ybf2=ybf2):
                ins = None
                for h in range(4):
                    ins = e.transpose(self.bank_bf(b)[:, h * 128:(h + 1) * 128], ybf2[:, h * 128:(h + 1) * 128],
                                      self.identb)
                return ins
            S.add("pe", trf, reads=[ybk, ("C", "identb")], writes=[bk])
            dst = dest_fn(s)
            srcv = self.bank_bf(b)[:, 0:512].rearrange("p (h t) -> p h t", t=128)
            S.add("act", lambda e, dst=dst, srcv=srcv: e.copy(dst, srcv), reads=[bk], writes=[dest_key])

    def post_v(self, banks, tile, vcol0, subs=None):
        S = self.S
        for s in (range(NS) if subs is None else subs):
            b = banks[s]
            vst = self.view(self.RX + 51200 + (s % 2) * 1024, 1024, BF16)
            vk = ("RX", "vst", s % 2)
            S.add("act", lambda e, vst=vst, b=b: e.copy(vst, self.bank(b)), reads=[("P", "bank", b)], writes=[vk])
            r0 = tile * T + s * 128
            dst = self.v_d[r0:r0 + 128, vcol0:vcol0 + 512]
            S.add("sp", lambda e, dst=dst, vst=vst: e.dma_start(out=dst, in_=vst), reads=[vk],
                  writes=[("VD", vcol0 // 512)], dma=("vst", s % 2))

    CG_QA = [0, 1, 2, 3]
    CG_KA = 4
    CG_VA = 5
    CG_QB = [6, 7, 8]
    CG_KB = [9, 10, 11]
    CG_VB = [12, 13, 14]

    h1_off = None

    def h1T_view(self, off=None):
        off = self.h1_off if off is None else off
        return self.view(off, 32768, BF16, "p (k t) -> p k t", t=T)

    def h1_key(self, off=None):
        off = self.h1_off if off is None else off
        return ("RA" if off == self.RA else "RB", "h1T")

    def compute_h1T(self, tile, off=None, subs=None, banks=None, phase="both"):
        if off is None:
            off = self.RA
        srcs = [("dram", self.x_seq[tile * T + s * 128: tile * T + (s + 1) * 128, :]) for s in range(NS)]
        self.norm_T(srcs, self.g1T, self.h1T_view(off), self.h1_key(off), self.RX, "n1", subs=subs, banks=banks,
                    phase=phase)

    def kv_pass(self, tile, hooks=None):
        h1T = self.h1T_view()
        act = lambda kc: h1T[:, kc, :]
        akeys = [self.h1_key()]
        kcount = [0]

        def k_job(cg, head0, gain_idx, rope, subs=None):
            ki = kcount[0] % 2
            kcount[0] += 1
            kst = self.view(self.RX + 43008 + ki * 4096, 4096, BF16, "p (h t) -> p h t", t=T)
            kk = ("RX", "kst", ki)
            sl = list(range(NS)) if subs is None else list(subs)
            c_lo, c_hi = sl[0] * 128, (sl[-1] + 1) * 128

            def post(banks):
                self.post_qk(banks, tile, gain_idx, rope,
                             lambda s: kst[:, :, s * 128:(s + 1) * 128], kk, subs=sl)
                dst = self.kt_d[head0:head0 + 4, :, tile * T + c_lo:tile * T + c_hi].rearrange("h p t -> p h t")
                self.S.add("sp", lambda e: e.dma_start(out=dst, in_=kst[:, :, c_lo:c_hi]), reads=[kk],
                           writes=[("KD", head0 // 4)], dma=("kst", ki))
            self.job(lambda: self.stream_tok(self.w_in, cg * 512, 512, KC, act, akeys, subs=subs), post)

        def v_job(cg, vcol0, subs=None):
            self.job(lambda: self.stream_tok(self.w_in, cg * 512, 512, KC, act, akeys, subs=subs),
                     lambda banks: self.post_v(banks, tile, vcol0, subs=subs))

        gsubs = {0: None, 1: None, 2: None}
        if tile in self.B_SUBS:
            gsubs = {0: self.B_SUBS[tile][0], 1: self.B_SUBS[tile][1], 2: None}
        jobs = [lambda: k_job(self.CG_KA, 0, 1, True), lambda: v_job(self.CG_VA, 0)]
        for g in range(3):
            jobs.append(lambda g=g: k_job(self.CG_KB[g], 4 + 4 * g, 3, False, subs=gsubs[g]))
            jobs.append(lambda g=g: v_job(self.CG_VB[g], 512 * (g + 1), subs=gsubs[g]))
        for i, j in enumerate(jobs):
            j()
            if hooks and i in hooks:
                hooks[i]()

    B_SUBS = {2: {0: [0], 1: [0, 1]}, 3: {0: [3], 1: [2, 3]}}

    def free_set_banks(self):
        s_ = self.set_i % 2
        return [4 * s_ + j for j in range(4)]

    def q_pass(self, tile):
        h1T = self.h1T_view()
        act = lambda kc: h1T[:, kc, :]
        akeys = [self.h1_key()]
        QT = self.view(self.RB, 28672, BF16, "p (h t) -> p h t", t=T)
        allq = self.CG_QA + self.CG_QB
        for i in (0, 4, 1, 5, 2, 6, 3):
            cg = allq[i]
            head0 = 4 * i
            rope = i < 4
            gain_idx = 0 if rope else 2

            def post(banks, head0=head0, rope=rope, gain_idx=gain_idx):
                self.post_qk(banks, tile, gain_idx, rope,
                             lambda s: QT[:, head0:head0 + 4, s * 128:(s + 1) * 128], ("RB", "QT"))
            self.job(lambda cg=cg: self.stream_tok(self.w_in, cg * 512, 512, KC, act, akeys), post)

    def attention(self, tile):
        S = self.S
        self.flush()
        S.barrier("RX")
        S.barrier("RT")
        RX, RT = self.RX, self.RT
        QT = self.view(self.RB, 28672, BF16, "p (h t) -> p h t", t=T)
        oT = self.view(RX + 45056, 20480, BF16, "p (h t) -> p h t", t=T)
        kt = [self.view(RX + i * 4096, 4096, BF16) for i in range(2)]
        vv = [self.view(RX + 8192 + i * 4096, 4096, BF16, "p (c d) -> p c d", d=128) for i in range(2)]
        strips = [self.view(RX + 16384, 7680, F32), self.view(RX + 24064, 7680, F32)]
        accO = self.view(RX + 31744, 8192, F32, "p (s t) -> p s t", t=T)
        accD = self.view(RT, 8192, F32, "p (s t) -> p s t", t=T)
        NP_, NT_ = 4, 3
        P = [self.view(RT + 8192 + i * 1024, 1024, BF16) for i in range(NP_)]
        tmp = [self.view(RT + 12288 + i * 2048, 2048, F32) for i in range(NT_)]
        st = {"kv": 0, "p": 0, "t": 0, "s": 0, "od": 0}
        q0 = tile * T

        def load_kv(kvidx, vcol, chunks=None):
            sl = st["kv"] % 2
            st["kv"] += 1
            cs = list(range(16)) if chunks is None else sorted(chunks)
            runs = []
            for c in cs:
                if runs and runs[-1][1] == c - 1:
                    runs[-1][1] = c
                else:
                    runs.append([c, c])
            for (a, b_) in runs:
                S.add("sp", lambda e, a=a, b_=b_: e.dma_start(out=kt[sl][:, a * 128:(b_ + 1) * 128],
                                                            in_=self.kt_d[kvidx][:, a * 128:(b_ + 1) * 128]),
                      reads=[("KD", kvidx // 4)], writes=[("RX", "kt", sl)], dma=("kt", sl))
                src = self.v_d[a * 128:(b_ + 1) * 128, vcol:vcol + 128].rearrange("(c p) d -> p c d", p=128)
                S.add("sp", lambda e, a=a, b_=b_, src=src: e.dma_start(out=vv[sl][:, a:b_ + 1, :], in_=src),
                      reads=[("VD", vcol // 512)], writes=[("RX", "v", sl)], dma=("v", sl))
            return sl

        def load_strip(bh, w):
            g_ = bh // 4
            offs = [q0 - c * 128 + (C_OWN if w == 0 else C_OTH) for c in needed_chunks(g_) if (c >= 8) == (w == 1)]
            if not offs:
                return
            lo, hi = min(offs), max(offs) + T
            S.add("sp", lambda e: e.dma_start(out=strips[w][:, lo:hi], in_=self.c_strip[w, bh][:, lo:hi]),
                  writes=[("RX", "strip", w)], dma=("strip", w))

        items = []
        deferred = []

        def add_head(qidx, kvload, bh, epilogue, chunks):
            ctx = {"qidx": qidx, "kvload": kvload, "bh": bh, "epi": epilogue, "sl": None, "sb": {}}
            for i, c in enumerate(chunks):
                items.append((ctx, c, i == 0, i == len(chunks) - 1))

        _nc_cache = {}

        def needed_chunks(g):
            if g in _nc_cache:
                return _nc_cache[g]
            idx = _STRIP_IDX
            out = []
            for c in range(16):
                w = 0 if c < 8 else 1
                off = q0 - c * 128 + (C_OWN if w == 0 else C_OTH)
                need = False
                for hh in range(2):
                    if (idx[(w, hh, g)][:, off:off + T] != 32).any():
                        need = True
                if need:
                    out.append(c)
            if g in (0, 1):
                for c in out:
                    if c >= 8:
                        assert (c % 4) in self.B_SUBS[c // 4][g], (g, c)
            _nc_cache[g] = out
            return out

        for kv in range(4):
            for g in range(4):
                h = kv * 4 + g

                def epi(ob, db, h=h):
                    ti = st["t"] % NT_
                    st["t"] += 1
                    S.add("dve", lambda e: e.reciprocal(tmp[ti], self.bank(db)),
                          reads=[("P", "bank", db)], writes=[("RT", "tmp", ti)])
                    S.add("dve", lambda e: e.tensor_tensor(oT[:, h, :], self.bank(ob), tmp[ti], ALU.mult),
                          reads=[("P", "bank", ob), ("RT", "tmp", ti)], writes=[("RX", "oT")])
                add_head(h, (kv, kv * 128) if g == 0 else None, None, epi, list(range(16)))
        for g in range(3):
            chunks = needed_chunks(g)
            for s4 in range(4):
                bh = 4 * g + s4

                def epi(ob, db, g=g, s4=s4):
                    ak = ("RX", "acc", s4)
                    if g == 0:
                        S.add("dve", lambda e: e.tensor_copy(accO[:, s4, :], self.bank(ob)),
                              reads=[("P", "bank", ob)], writes=[ak])
                        S.add("dve", lambda e: e.tensor_copy(accD[:, s4, :], self.bank(db)),
                              reads=[("P", "bank", db)], writes=[("RT", "accD", s4)])
                    else:
                        S.add("dve", lambda e: e.tensor_tensor(accO[:, s4, :], self.bank(ob), accO[:, s4, :], ALU.add),
                              reads=[("P", "bank", ob), ak], writes=[ak])
                        S.add("dve", lambda e: e.tensor_tensor(accD[:, s4, :], self.bank(db), accD[:, s4, :], ALU.add),
                              reads=[("P", "bank", db), ("RT", "accD", s4)], writes=[("RT", "accD", s4)])
                    if g == 2:
                        def fin(s4=s4, ak=ak):
                            S.add("dve", lambda e: e.reciprocal(accD[:, s4, :], accD[:, s4, :]),
                                  reads=[("RT", "accD", s4)], writes=[("RT", "accD", s4)])
                            S.add("dve", lambda e: e.tensor_tensor(oT[:, 16 + s4, :], accO[:, s4, :], accD[:, s4, :], ALU.mult),
                                  reads=[ak, ("RT", "accD", s4)], writes=[("RX", "oT")])
                        deferred.append(fin)
                add_head(16 + bh, (4 + bh, 512 * (g + 1) + s4 * 128, tuple(chunks)), bh, epi, chunks)

        loads = []
        for (ctx, c, first, last) in items:
            if first:
                if ctx["kvload"] is not None:
                    loads.append(ctx["kvload"])
                ctx["li"] = len(loads) - 1
                ctx["sl"] = ctx["li"] % 2
        issued = [0]

        def ensure_loads(upto):
            while issued[0] <= min(upto, len(loads) - 1):
                load_kv(*loads[issued[0]])
                issued[0] += 1

        def issue_S(i):
            ctx, c, first, last = items[i]
            if first:
                ensure_loads(ctx["li"])
            sl = ctx["sl"]
            qidx = ctx["qidx"]
            sb = st["s"] % 4
            st["s"] += 1
            ctx["sb"][c] = sb
            S.add("pe", lambda e: e.matmul(self.bank(sb), kt[sl][:, c * 128:(c + 1) * 128], QT[:, qidx, :],
                                           start=True, stop=True),
                  reads=[("RX", "kt", sl), ("RB", "QT")], writes=[("P", "bank", sb)])

        LOOK = 3
        for i in range(min(LOOK, len(items))):
            issue_S(i)
        for i in range(len(items)):
            if i + LOOK < len(items):
                issue_S(i + LOOK)
            ctx, c, first, last = items[i]
            if first:
                pair = st["od"] % 2
                st["od"] += 1
                ctx["ob"], ctx["db"] = 4 + 2 * pair, 5 + 2 * pair
                if ctx["kvload"] is not None:
                    ensure_loads(ctx["li"] + 1)
                if ctx["bh"] is not None:
                    if not st.get("s0_loaded", False):
                        load_strip(ctx["bh"], 0)
                    st["s0_loaded"] = False
                    load_strip(ctx["bh"], 1)
            ob, db, sl, bh = ctx["ob"], ctx["db"], ctx["sl"], ctx["bh"]
            sb = ctx["sb"][c]
            pi = st["p"] % NP_
            st["p"] += 1
            if bh is None:
                S.add("act", lambda e, sb=sb, pi=pi: e.activation(P[pi], self.bank(sb), AF.Exp, scale=SCALE),
                      reads=[("P", "bank", sb)], writes=[("RT", "P", pi)])
            else:
                ti = st["t"] % NT_
                st["t"] += 1
                w = 0 if c < 8 else 1
                off = q0 - c * 128 + (C_OWN if w == 0 else C_OTH)
                assert 0 <= off and off + T <= SW_OWN
                sv = strips[w][:, off:off + T]
                S.add("dve", lambda e, sb=sb, ti=ti, sv=sv: e.scalar_tensor_tensor(
                    tmp[ti], self.bank(sb), SCALE, sv, ALU.mult, ALU.add),
                    reads=[("P", "bank", sb), ("RX", "strip", w)], writes=[("RT", "tmp", ti)])
                S.add("act", lambda e, ti=ti, pi=pi: e.activation(P[pi], tmp[ti], AF.Exp),
                      reads=[("RT", "tmp", ti)], writes=[("RT", "P", pi)])

            def pv(e, c=c, pi=pi, ob=ob, db=db, sl=sl, first=first, last=last):
                e.matmul(self.bank(ob), vv[sl][:, c, :], P[pi], start=first, stop=last)
                return e.matmul(self.bank(db), self.onesb, P[pi], start=first, stop=last)
            S.add("pe", pv, reads=[("RX", "v", sl), ("RT", "P", pi), ("C", "onesb")],
                  writes=[("P", "bank", ob), ("P", "bank", db)])
            if bh is not None and c < 8:
                rest = [items[j][1] for j in range(i + 1, len(items)) if items[j][0] is ctx]
                if not any(cc < 8 for cc in rest):
                    nxt = [items[j][0] for j in range(i + 1, len(items)) if items[j][0] is not ctx]
                    if nxt and nxt[0]["bh"] is not None:
                        load_strip(nxt[0]["bh"], 0)
                        st["s0_loaded"] = True
            if last:
                ctx["epi"](ob, db)
        for fin in deferred:
            fin()

    def merge(self, tile):
        S = self.S
        self.flush()
        S.barrier("RB")
        S.barrier("RX")
        S.barrier("RT")
        RX = self.RX
        h1T = self.h1T_view()
        oT = self.view(RX + 45056, 20480, BF16, "p (h t) -> p h t", t=T)
        mergedT = self.view(self.RB, 32768, BF16, "p (k t) -> p k t", t=T)
        sa = self.view(RX, 8192, F32, "p (j t) -> p j t", t=T)
        sb_ = self.view(RX + 8192, 8192, F32, "p (j t) -> p j t", t=T)
        mm = self.view(RX + 16384, 8192, F32, "p (j t) -> p j t", t=T)
        tt = self.view(RX + 24576, 2048, F32)
        hk = [("RA", "h1T")]
        ok = [("RX", "oT")]
        for cg in range(8):
            def post_ga(banks, cg=cg):
                for j in range(4):
                    bias = self.bgT[:, cg * 4 + j:cg * 4 + j + 1]
                    S.add("act", lambda e, j=j, b=banks[j], bias=bias: e.activation(sa[:, j, :], self.bank(b), AF.Sigmoid, bias=bias),
                          reads=[("P", "bank", banks[j]), ("C", "sm")], writes=[("RX", "sa")])
            self.job(lambda cg=cg: self.stream_feat(self.w_in, 0, 7680 + cg * 512, KC, lambda kc: h1T[:, kc, :], hk), post_ga)

            def post_pa(banks):
                for j in range(4):
                    S.add("dve", lambda e, j=j, b=banks[j]: e.tensor_tensor(mm[:, j, :], self.bank(b), sa[:, j, :], ALU.mult),
                          reads=[("P", "bank", banks[j]), ("RX", "sa")], writes=[("RX", "mm")])
            self.job(lambda cg=cg: self.stream_feat(self.w_pa, 0, cg * 512, 16, lambda kc: oT[:, kc, :], ok), post_pa)

            def post_gb(banks, cg=cg):
                for j in range(4):
                    bias = self.bgT[:, 32 + cg * 4 + j:32 + cg * 4 + j + 1]
                    S.add("act", lambda e, j=j, b=banks[j], bias=bias: e.activation(sb_[:, j, :], self.bank(b), AF.Sigmoid, bias=bias),
                          reads=[("P", "bank", banks[j]), ("C", "sm")], writes=[("RX", "sb")])
            self.job(lambda cg=cg: self.stream_feat(self.w_in, 0, 11776 + cg * 512, KC, lambda kc: h1T[:, kc, :], hk), post_gb)

            def post_pb(banks, cg=cg):
                for j in range(4):
                    S.add("dve", lambda e, j=j, b=banks[j]: e.tensor_tensor(tt, self.bank(b), sb_[:, j, :], ALU.mult),
                          reads=[("P", "bank", banks[j]), ("RX", "sb")], writes=[("RX", "tt")])
                    S.add("dve", lambda e, j=j: e.tensor_tensor(mergedT[:, cg * 4 + j, :], tt, mm[:, j, :], ALU.add),
                          reads=[("RX", "tt"), ("RX", "mm")], writes=[("RB", "mergedT")])
            self.job(lambda cg=cg: self.stream_feat(self.w_pb, 0, cg * 512, 4, lambda kc: oT[:, 16 + kc, :], ok), post_pb)

    def wout(self, tile):
        S = self.S
        self.flush()
        S.barrier("RA")
        S.barrier("RX")
        mergedT = self.view(self.RB, 32768, BF16, "p (k t) -> p k t", t=T)
        x1 = self.view(self.RX, 65536, F32, "p (s d) -> p s d", d=D)
        xres = [self.view(self.RA + i * 8192, 8192, F32, "p (s c) -> p s c", c=512) for i in range(2)]
        for cg in range(8):
            i = cg % 2
            src = self.x_seq[tile * T:(tile + 1) * T, cg * 512:(cg + 1) * 512].rearrange("(s p) c -> p s c", p=128)
            S.add("sp", lambda e, i=i, src=src: e.dma_start(out=xres[i], in_=src), writes=[("RA", "xres", i)], dma=("xres", i))

            def post(banks, cg=cg, i=i):
                for s in range(NS):
                    S.add("dve", lambda e, s=s, b=banks[s]: e.tensor_tensor(x1[:, s, cg * 512:(cg + 1) * 512], self.bank(b), xres[i][:, s, :], ALU.add),
                          reads=[("P", "bank", banks[s]), ("RA", "xres", i)], writes=[("RX", "x1", s)])
            self.job(lambda cg=cg: self.stream_tok(self.w_out, cg * 512, 512, KC, lambda kc: mergedT[:, kc, :], [("RB", "mergedT")]), post)

    def ffn(self, tile):
        S = self.S
        self.flush()
        S.barrier("RA")
        S.barrier("RB")
        S.barrier("RT")
        x1 = self.view(self.RX, 65536, F32, "p (s d) -> p s d", d=D)
        h2T = self.view(self.RB, 32768, BF16, "p (k t) -> p k t", t=T)
        srcs = [("sbuf", x1[:, s, :], [("RX", "x1", s)]) for s in range(NS)]
        self.norm_T(srcs, self.g2T, h2T, ("RB", "h2T"), self.RA, "n2")
        S.barrier("RA")
        sg = [self.view(self.RA + i * 8192, 8192, F32, "p (j t) -> p j t", t=T) for i in range(2)]
        aT = [self.view(self.RA + 16384 + i * 4096, 4096, BF16, "p (j t) -> p j t", t=T) for i in range(2)]
        hk = [("RB", "h2T")]
        gsz = [4] * 20 + [3, 3]
        gst = [sum(gsz[:i]) for i in range(len(gsz))]
        NG = len(gsz)
        assert sum(gsz) * 128 == DFF

        def njof(i):
            return gsz[i]

        def GU(i):
            nj = njof(i)
            p = i % 2

            def post_g(banks):
                for j in range(nj):
                    S.add("act", lambda e, j=j, b=banks[j]: e.activation(sg[p][:, j, :], self.bank(b), AF.Silu),
                          reads=[("P", "bank", banks[j])], writes=[("RA", "sg", p)])
            self.job(lambda: self.stream_feat(self.w_g, 0, gst[i] * 128, KC, lambda kc: h2T[:, kc, :], hk, nj=nj), post_g)

            def post_u(banks):
                for j in range(nj):
                    S.add("dve", lambda e, j=j, b=banks[j]: e.tensor_tensor(aT[p][:, j, :], self.bank(b), sg[p][:, j, :], ALU.mult),
                          reads=[("P", "bank", banks[j]), ("RA", "sg", p)], writes=[("RA", "aT", p)])
            self.job(lambda: self.stream_feat(self.w_u, 0, gst[i] * 128, KC, lambda kc: h2T[:, kc, :], hk, nj=nj), post_u)

        def Dn(i):
            nj = njof(i)
            p = i % 2
            for cg in range(8):
                def post(banks, cg=cg):
                    for s in range(NS):
                        xs_ = x1[:, s, cg * 512:(cg + 1) * 512]
                        S.add("dve", lambda e, b=banks[s], xs_=xs_: e.tensor_tensor(xs_, self.bank(b), xs_, ALU.add),
                              reads=[("P", "bank", banks[s]), ("RX", "x1", s)], writes=[("RX", "x1", s)])
                    if i == NG - 1:
                        dst = self.out[tile * T:(tile + 1) * T, cg * 512:(cg + 1) * 512].rearrange("(s p) c -> p s c", p=128)
                        o = S.add("sp", lambda e: e.dma_start(out=dst, in_=x1[:, :, cg * 512:(cg + 1) * 512]),
                                  reads=[("RX", "x1", s_) for s_ in range(NS)], dma=("out", cg % 4))
                        self.final.append(o)
                self.job(lambda cg=cg: self.stream_tok(self.w_d, cg * 512, 512, nj, lambda kc: aT[p][:, kc, :],
                                                       [("RA", "aT", p)], krow0=gst[i]), post)
        GU(0)
        for i in range(NG):
            if i + 1 < NG:
                GU(i + 1)
            Dn(i)
        self.flush()

    def build(self):
        self.setup()
        self.load_consts()
        S = self.S
        order = (2, 3, 1, 0)
        bufs = {2: self.RB, 3: self.RA, 1: self.RB, 0: self.RA}
        self.compute_h1T(order[0], off=bufs[order[0]])
        for i, tile in enumerate(order):
            self.h1_off = bufs[tile]
            hooks = None
            if i + 1 < len(order):
                nt = order[i + 1]
                def h3(nt=nt):
                    self.compute_h1T(nt, off=bufs[nt], subs=(0, 1), banks=self.free_set_banks(), phase="back")
                    self.compute_h1T(nt, off=bufs[nt], subs=(2, 3), phase="front")
                hooks = {
                    1: lambda nt=nt: self.compute_h1T(nt, off=bufs[nt], subs=(0, 1), phase="front"),
                    3: h3,
                    5: lambda nt=nt: self.compute_h1T(nt, off=bufs[nt], subs=(2, 3), banks=self.free_set_banks(),
                                                      phase="back"),
                }
            self.kv_pass(tile, hooks)
        self.h1_off = self.RA
        S.barrier("RB")
        for tile in (0, 1):
            if tile == 1:
                self.flush()
                for r in ("RA", "RB", "RX", "RT"):
                    S.barrier(r)
                self.compute_h1T(1)
            self.q_pass(tile)
            self.attention(tile)
            self.merge(tile)
            self.wout(tile)
            self.ffn(tile)
        S.emit(self.nc, self.es, self.final)
        return self.nc


def xs_off_region(self, off):
    return "RX" if off >= self.RX and off < self.RT else ("RA" if off < self.RB else "RB")


def _t5_bucket_np(rel):
    nb = 16
    max_exact = 8
    rel = np.asarray(rel, dtype=np.int64)
    side = np.where(rel > 0, nb, 0)
    n = np.abs(rel)
    nf = np.maximum(n, 1).astype(np.float32)
    large = max_exact + (np.log(nf / np.float32(max_exact)) / np.float32(math.log(1024 / max_exact))
                         * np.float32(nb - max_exact)).astype(np.int32)
    large = np.minimum(large, nb - 1)
    return side + np.where(n < max_exact, n, large)


_B_PATTERNS = ((128, 1), (512, 4), (2048, 16))


def _strip_index_tables():
    out = {}
    kp = np.arange(128)[:, None]
    m = np.arange(SW_OWN)[None, :]
    for hh in range(2):
        for which in range(2):
            if which == 0:
                delta = kp - m + C_OWN
            else:
                delta = kp - m + C_OTH - hh * 2048
            for g, (window, dil) in enumerate(_B_PATTERNS):
                ok = (delta % dil == 0) & (np.abs(delta) <= (window // (2 * dil)) * dil)
                idx = np.where(ok, _t5_bucket_np(delta), 32)
                out[(which, hh, g)] = idx
    return out


_STRIP_IDX = _strip_index_tables()


def _rope_table(hh):
    pos = np.arange(SEQ)
    actual = np.where(pos < 1024, hh * 1024 + pos, (1 - hh) * 1024 + (pos - 1024))
    row = (actual // 64).astype(np.float32)
    col = (actual % 64).astype(np.float32)
    inv = (np.float32(10000.0) ** (-np.arange(0, 64, 2, dtype=np.float32) / np.float32(64))).astype(np.float32)
    ar = row[:, None] * inv[None, :]
    ac = col[:, None] * inv[None, :]
    tab = np.stack([np.cos(ar), np.cos(ac), np.sin(ar), np.sin(ac)], axis=1).astype(np.float32)
    tab = tab.reshape(16, 128, 4, 32).transpose(1, 0, 2, 3).reshape(128, 16 * 4 * 32)
    return np.ascontiguousarray(tab)


def prep_core_inputs(inputs, core):
    b, hh = core // 2, core % 2
    f32 = np.float32
    x = np.asarray(inputs["x"], dtype=f32)
    own = x[b, hh * 1024:(hh + 1) * 1024]
    oth = x[b, (1 - hh) * 1024:(2 - hh) * 1024]
    x_seq = np.ascontiguousarray(np.concatenate([own, oth], axis=0))
    cs = np.zeros((128, 1024), dtype=f32)
    cs[:, 0:128] = np.eye(128, dtype=f32)
    cs[:, 128:160] = np.asarray(inputs["norm1_g"], f32)[0].reshape(32, 128).T
    cs[:, 160:192] = np.asarray(inputs["norm2_g"], f32)[0].reshape(32, 128).T
    bg = np.asarray(inputs["b_gate"], f32)[0]
    cs[:, 192:224] = bg[0].reshape(32, 128).T
    cs[:, 224:256] = bg[1].reshape(32, 128).T
    for i, nm in enumerate(["q_norm_a", "k_norm_a", "q_norm_b", "k_norm_b"]):
        cs[:, 256 + i * 128:256 + (i + 1) * 128] = np.asarray(inputs[nm], f32)[0][None, :]
    rb = np.asarray(inputs["rel_bias"], f32)
    rb_ext = np.concatenate([rb, np.full((1, rb.shape[1]), NEG, dtype=f32)], axis=0)
    idx = _strip_index_tables()
    strips = np.empty((2, 12, 128, SW_OWN), dtype=f32)
    for which in range(2):
        for g in range(3):
            ii = idx[(which, hh, g)]
            for s in range(4):
                strips[which, 4 * g + s] = rb_ext[ii, 4 * g + s]
    m = {
        "x_seq": x_seq,
        "w_in": np.asarray(inputs["w_in"], f32)[0],
        "w_proj_a": np.asarray(inputs["w_proj_a"], f32)[0],
        "w_proj_b": np.asarray(inputs["w_proj_b"], f32)[0],
        "w_out": np.asarray(inputs["w_out"], f32)[0],
        "w_ffn_gate": np.asarray(inputs["w_ffn_gate"], f32)[0],
        "w_ffn_up": np.asarray(inputs["w_ffn_up"], f32)[0],
        "w_ffn_down": np.asarray(inputs["w_ffn_down"], f32)[0],
        "c_small": cs,
        "c_rope": _rope_table(hh),
        "c_strip": strips,
    }
    return m


_CACHE = {}


def kernel(**inputs):
    if "nc" not in _CACHE:
        _CACHE["nc"] = Builder().build()
    nc = _CACHE["nc"]
    in_maps = [prep_core_inputs(inputs, c) for c in range(8)]
    res = run_bass_kernel_spmd(nc, in_maps, core_ids=list(range(8)))
    out = np.empty((BATCH, SEQ, D), dtype=np.float32)
    for c in range(8):
        b, hh = c // 2, c % 2
        out[b, hh * 1024:(hh + 1) * 1024] = np.asarray(res.results[c]["out"], dtype=np.float32)
    return out
```

```python
import math
from contextlib import ExitStack

import numpy as np
import concourse.bass as bass
import concourse.mybir as mybir
from concourse.bass_utils import run_bass_kernel_spmd

F32 = mybir.dt.float32
BF16 = mybir.dt.bfloat16
AF = mybir.ActivationFunctionType
ALU = mybir.AluOpType
AX = mybir.AxisListType

D = 4096
SEQ = 2048
BATCH = 4
HD = 128
DFF = 11008
T = 512
NS = 4
KC = D // 128
EPS = 1e-6
SCALE = HD ** -0.5
IN_W = 15872
NEG = -1e30
SW_OWN = 1920
SW_OTH = 1920
C_OWN = 896
C_OTH = 1920

NWSLOT = 6


class Op:
    __slots__ = ("eng", "fn", "deps", "signal", "val", "is_dma", "sem")


class Sched:
    ENGS = ("pe", "act", "dve", "pool", "sp")

    def __init__(self):
        self.ops = {e: [] for e in self.ENGS}
        self.state = {}
        self.dma_cnt = {}
        self.pending = {}
        self.uid = 0

    def _st(self, k):
        st = self.state.get(k)
        if st is None:
            st = {"w": None, "r": {}}
            self.state[k] = st
        return st

    def barrier(self, region):
        pend = []
        for k, st in self.state.items():
            if k[0] == region:
                if st["w"] is not None:
                    pend.append(st["w"])
                pend.extend(st["r"].values())
                st["w"] = None
                st["r"] = {}
        old = self.pending.get(region, [])
        self.pending[region] = list(set(pend + old))

    def add(self, eng, fn, reads=(), writes=(), dma=None):
        o = Op()
        o.eng = eng
        o.fn = fn
        o.is_dma = dma is not None
        o.signal = False
        o.val = 0
        o.sem = None
        deps = []
        for k in reads:
            st = self._st(k)
            if st["w"] is not None:
                deps.append(st["w"])
            if k[0] in self.pending:
                deps.extend(self.pending[k[0]])
        for k in writes:
            st = self._st(k)
            if st["w"] is not None:
                deps.append(st["w"])
            deps.extend(st["r"].values())
            if k[0] in self.pending:
                deps.extend(self.pending[k[0]])
        dd = []
        seen = set()
        for d in deps:
            if id(d) in seen:
                continue
            seen.add(id(d))
            if (not d.is_dma) and (not o.is_dma) and d.eng == "pe" and eng == "pe":
                continue
            d.signal = True
            dd.append(d)
        o.deps = dd
        if o.is_dma:
            n = self.dma_cnt.get(dma, 0) + 1
            self.dma_cnt[dma] = n
            o.sem = ("dma", dma)
            o.val = 16 * n
        self.uid += 1
        rkey = eng if not o.is_dma else ("dma", self.uid)
        for k in reads:
            self._st(k)["r"][rkey] = o
        for k in writes:
            st = self._st(k)
            st["w"] = o
            st["r"] = {}
        self.ops[eng].append(o)
        return o

    def emit(self, nc, es, final_waits):
        for e in self.ENGS:
            cnt = 0
            for o in self.ops[e]:
                if not o.is_dma and o.signal:
                    cnt += 1
                    o.val = cnt
                    o.sem = ("eng", e)
        sems = {}

        def sem_of(key):
            if key not in sems:
                nm = "s_" + "_".join(str(x) for x in key).replace(" ", "")
                sems[key] = es.enter_context(nc.semaphore(nm))
            return sems[key]

        for e in self.ENGS:
            for o in self.ops[e]:
                if o.sem is not None:
                    sem_of(o.sem)
        block = es.enter_context(nc.Block())

        def run(e):
            def body(engine):
                waited = {}
                for o in self.ops[e]:
                    for d in o.deps:
                        if waited.get(d.sem, 0) < d.val:
                            engine.wait_ge(sems[d.sem], d.val)
                            waited[d.sem] = d.val
                    ins = o.fn(engine)
                    if o.is_dma:
                        ins.then_inc(sems[o.sem], 16)
                    elif o.signal:
                        ins.then_inc(sems[o.sem], 1)
                if e == "sp":
                    for o in final_waits:
                        engine.wait_ge(sems[o.sem], o.val)
            return body

        block.tensor(run("pe"))
        block.scalar(run("act"))
        block.vector(run("dve"))
        block.gpsimd(run("pool"))
        block.sync(run("sp"))


class Builder:
    def __init__(self, debug=None):
        self.debug = debug or {}
        self.nc = bass.Bass("TRN2", target_bir_lowering=False)
        self.S = Sched()
        self.es = ExitStack()
        self.final = []
        self.wload_i = 0
        self.set_i = 0

    def setup(self):
        nc = self.nc
        dt = nc.dram_tensor
        self.x_seq = dt("x_seq", [SEQ, D], F32, kind="ExternalInput").ap()
        self.w_in = dt("w_in", [D, IN_W], F32, kind="ExternalInput").ap()
        self.w_pa = dt("w_proj_a", [2048, D], F32, kind="ExternalInput").ap()
        self.w_pb = dt("w_proj_b", [512, D], F32, kind="ExternalInput").ap()
        self.w_out = dt("w_out", [D, D], F32, kind="ExternalInput").ap()
        self.w_g = dt("w_ffn_gate", [D, DFF], F32, kind="ExternalInput").ap()
        self.w_u = dt("w_ffn_up", [D, DFF], F32, kind="ExternalInput").ap()
        self.w_d = dt("w_ffn_down", [DFF, D], F32, kind="ExternalInput").ap()
        self.c_small = dt("c_small", [128, 1024], F32, kind="ExternalInput").ap()
        self.c_rope = dt("c_rope", [128, 16 * 4 * 32], F32, kind="ExternalInput").ap()
        self.c_strip = dt("c_strip", [2, 12, 128, SW_OWN], F32, kind="ExternalInput").ap()
        self.out = dt("out", [1024, D], F32, kind="ExternalOutput").ap()
        self.kt_d = dt("kt_scratch", [16, 128, SEQ], BF16, kind="Internal").ap()
        self.v_d = dt("v_scratch", [SEQ, 2048], BF16, kind="Internal").ap()
        self.dbg = {}
        for name, shape in self.debug.items():
            self.dbg[name] = dt("dbg_" + name, list(shape), F32, kind="ExternalOutput").ap()

        ARENA = 52992
        self.arena = self.es.enter_context(nc.sbuf_tensor("arena", [128, ARENA], F32))
        self.psum = self.es.enter_context(nc.psum_tensor("psum", [128, 8, 512], F32))

    def view(self, byte_off, nbytes, dtype, pattern=None, **kw):
        assert byte_off % 4 == 0 and nbytes % 4 == 0
        v = self.arena[:, byte_off // 4:(byte_off + nbytes) // 4]
        if dtype == BF16:
            v = v.bitcast(BF16)
        if pattern:
            v = v.rearrange(pattern, **kw)
        return v

    def bank(self, i):
        return self.psum[:, i, :]

    def bank_bf(self, i):
        return self.psum[:, i, :].bitcast(BF16)

    def wslot_view(self, i):
        return self.view(i * 8192, 8192, BF16, "p (a b) -> p a b", b=512)

    def load_w(self, wd, k0, nk, c0, ncol):
        i = self.wload_i % NWSLOT
        self.wload_i += 1
        dst = self.wslot_view(i)[:, 0:nk, 0:ncol]
        src = wd[k0 * 128:(k0 + nk) * 128, c0:c0 + ncol].rearrange("(kc p) n -> p kc n", p=128)
        self.S.add("pool", lambda e, dst=dst, src=src: e.dma_start(out=dst, in_=src),
                   reads=(), writes=(("W", "slot", i),), dma=("w", i))
        return i

    def next_set(self):
        s = self.set_i % 2
        self.set_i += 1
        return [4 * s + j for j in range(4)]

    def stream_tok(self, wd, c0, ncol, nkc, act_fn, act_keys, kgran=8, krow0=0, subs=None):
        banks = self.next_set()
        k = 0
        while k < nkc:
            nk = min(kgran, nkc - k)
            slot = self.load_w(wd, krow0 + k, nk, c0, ncol)
            wv = self.wslot_view(slot)

            def fn(e, k=k, nk=nk, wv=wv):
                ins = None
                for s in (range(NS) if subs is None else subs):
                    for kk in range(nk):
                        ins = e.matmul(self.bank(banks[s])[:, 0:ncol],
                                       act_fn(k + kk)[:, s * 128:(s + 1) * 128],
                                       wv[:, kk, 0:ncol],
                                       start=(k + kk == 0), stop=(k + kk == nkc - 1))
                return ins
            self.S.add("pe", fn, reads=[("W", "slot", slot)] + list(act_keys),
                       writes=[("P", "bank", b) for b in banks])
            k += nk
        return banks

    def stream_feat(self, wd, krow0, c0, nkc, act_fn, act_keys, nj=4, kgran=8):
        banks = self.next_set()
        k = 0
        while k < nkc:
            nk = min(kgran, nkc - k)
            slot = self.load_w(wd, krow0 + k, nk, c0, nj * 128)
            wv = self.wslot_view(slot)

            def fn(e, k=k, nk=nk, wv=wv):
                ins = None
                for j in range(nj):
                    for kk in range(nk):
                        ins = e.matmul(self.bank(banks[j]),
                                       wv[:, kk, j * 128:(j + 1) * 128],
                                       act_fn(k + kk),
                                       start=(k + kk == 0), stop=(k + kk == nkc - 1))
                return ins
            self.S.add("pe", fn, reads=[("W", "slot", slot)] + list(act_keys),
                       writes=[("P", "bank", b) for b in banks[:nj]])
            k += nk
        return banks

    CB = NWSLOT * 8192

    def load_consts(self):
        S = self.S
        cb = self.CB
        self.c_sm = self.view(cb, 4096, F32)
        self.c_rp = self.view(cb + 4096, 8192, F32, "p (t a f) -> p t a f", a=4, f=32)
        self.identb = self.view(cb + 12288, 256, BF16)
        self.onesb = self.view(cb + 12544, 256, BF16)
        self.CEND = cb + 12800
        S.add("sp", lambda e: e.dma_start(out=self.c_sm, in_=self.c_small), writes=[("C", "sm")], dma=("c", 0))
        S.add("sp", lambda e: e.dma_start(out=self.view(cb + 4096, 8192, F32), in_=self.c_rope),
              writes=[("C", "rope")], dma=("c", 1))
        self.identf = self.c_sm[:, 0:128]
        self.g1T = self.c_sm[:, 128:160]
        self.g2T = self.c_sm[:, 160:192]
        self.bgT = self.c_sm[:, 192:256]
        self.qkg = self.c_sm[:, 256:768]
        S.add("dve", lambda e: e.tensor_copy(self.identb, self.identf), reads=[("C", "sm")], writes=[("C", "identb")])
        S.add("dve", lambda e: e.memset(self.onesb, 1.0), writes=[("C", "onesb")])

    DYN = 62464
    RA = DYN
    RB = DYN + 32768
    RX = DYN + 65536
    RT = DYN + 131072

    def rr_bank(self):
        b = self._rr % 8
        self._rr += 1
        return b
    _rr = 0

    def job(self, mm_fn, post_fn):
        banks = mm_fn()
        self.flush()
        self._pending = (post_fn, banks)

    _pending = None

    def flush(self):
        if self._pending is not None:
            fn, banks = self._pending
            self._pending = None
            fn(banks)

    def norm_T(self, srcs, gT, dstT, dst_key, xs_off, tag, subs=None, banks=None, phase="both"):
        S = self.S
        junk = self.view(self.RT, 8192, BF16)
        ssv = self.view(self.RT + 8192, 256, F32)
        for s in (range(NS) if subs is None else subs):
            xs = self.view(xs_off + (s % 2) * 16384, 16384, F32)
            xk = (xs_off_region(self, xs_off), "xs", tag, s % 2)
            src = srcs[s]
            ssk = ("RT", "ss", s)
            rsk = ("RT", "rs", s)
            ss = ssv[:, 4 * s:4 * s + 1]
            rs = ssv[:, 4 * s + 1:4 * s + 2]
            if phase in ("both", "front"):
                if src[0] == "dram":
                    S.add("sp", lambda e, xs=xs, a=src[1]: e.dma_start(out=xs, in_=a),
                          writes=[xk], dma=("xs", s % 2))
                    inp, inkeys = xs, [xk]
                else:
                    inp, inkeys = src[1], list(src[2])
                S.add("act", lambda e, inp=inp, ss=ss: e.activation(junk, inp, AF.Square, accum_out=ss),
                      reads=inkeys, writes=[ssk, ("RT", "junk")])
                S.add("act", lambda e, ss=ss, rs=rs: e.activation(rs, ss, AF.Ln, scale=1.0 / D, bias=EPS),
                      reads=[ssk], writes=[rsk])
                S.add("act", lambda e, rs=rs: e.activation(rs, rs, AF.Exp, scale=-0.5),
                      reads=[rsk], writes=[rsk])
                S.add("dve", lambda e, inp=inp, xs=xs, rs=rs: e.tensor_scalar_mul(xs, inp, rs),
                      reads=inkeys + [rsk], writes=[xk])
            if phase == "front":
                continue
            for q in range(KC // 4):
                b = self.rr_bank() if banks is None else banks[q % len(banks)]
                S.add("pe", lambda e, b=b, q=q, xs=xs: self._tr4(e, b, q, xs), reads=[xk, ("C", "sm")],
                      writes=[("P", "bank", b)])
                dst = dstT[:, 4 * q:4 * q + 4, s * 128:(s + 1) * 128]
                g = gT[:, 4 * q:4 * q + 4].unsqueeze(2).broadcast_to([128, 4, 128])
                pv = self.bank(b).rearrange("p (a t) -> p a t", t=128)
                S.add("dve", lambda e, dst=dst, pv=pv, g=g: e.tensor_tensor(dst, pv, g, ALU.mult),
                      reads=[("P", "bank", b), ("C", "sm")], writes=[dst_key])

    def _tr4(self, e, b, q, xs):
        ins = None
        for a in range(4):
            kc = 4 * q + a
            ins = e.transpose(self.bank(b)[:, a * 128:(a + 1) * 128], xs[:, kc * 128:(kc + 1) * 128], self.identf)
        return ins

    def post_qk(self, banks, tile_abs, gain_idx, rope, dest_fn, dest_key, subs=None):
        S = self.S
        RXb = self.RX
        gain = self.qkg[:, gain_idx * 128:(gain_idx + 1) * 128].unsqueeze(1).broadcast_to([128, 4, 128])
        subs = list(range(NS)) if subs is None else list(subs)
        for s in subs:
            b = banks[s]
            bk = ("P", "bank", b)
            sq = self.view(RXb + 32768, 2048, F32, "p (h d) -> p h d", d=128)
            yn = self.view(RXb + 34816, 2048, F32, "p (h d) -> p h d", d=128)
            ybf_off = RXb + 53248 + s * 1024
            ybf = self.view(ybf_off, 1024, BF16, "p (h d) -> p h d", d=128)
            ybk = ("RX", "ybf", s)
            qs = self.view(self.RT + 8448, 64, F32)
            qss = qs[:, 0:4]
            qrs = qs[:, 4:8]
            pv = self.bank(b).rearrange("p (h d) -> p h d", d=128)
            S.add("act", lambda e, sq=sq, pv=pv: e.activation(sq, pv, AF.Square), reads=[bk], writes=[("RX", "sq")])
            S.add("dve", lambda e, sq=sq, qss=qss: e.tensor_reduce(qss, sq, AX.X, ALU.add),
                  reads=[("RX", "sq")], writes=[("RT", "qss")])
            S.add("act", lambda e, qss=qss, qrs=qrs: e.activation(qrs, qss, AF.Ln, scale=1.0 / HD, bias=EPS),
                  reads=[("RT", "qss")], writes=[("RT", "qrs")])
            S.add("act", lambda e, qrs=qrs: e.activation(qrs, qrs, AF.Exp, scale=-0.5),
                  reads=[("RT", "qrs")], writes=[("RT", "qrs")])
            rb = qrs.unsqueeze(2).broadcast_to([128, 4, 128])
            S.add("dve", lambda e, yn=yn, pv=pv, rb=rb: e.tensor_tensor(yn, pv, rb, ALU.mult),
                  reads=[bk, ("RT", "qrs")], writes=[("RX", "yn")])
            if not rope:
                S.add("dve", lambda e, ybf=ybf, yn=yn: e.tensor_tensor(ybf, yn, gain, ALU.mult),
                      reads=[("RX", "yn"), ("C", "sm")], writes=[ybk])
            else:
                S.add("dve", lambda e, yn=yn: e.tensor_tensor(yn, yn, gain, ALU.mult),
                      reads=[("RX", "yn"), ("C", "sm")], writes=[("RX", "yn")])
                ts = tile_abs * NS + s
                cosb = self.c_rp[:, ts, 0:2, :].unsqueeze(1).broadcast_to([128, 4, 2, 32])
                sinb = self.c_rp[:, ts, 2:4, :].unsqueeze(1).broadcast_to([128, 4, 2, 32])
                y5 = self.view(RXb + 34816, 2048, F32, "p (h a t f) -> p h a t f", a=2, t=2, f=32)
                o5 = self.view(ybf_off, 1024, BF16, "p (h a t f) -> p h a t f", a=2, t=2, f=32)
                x1v = y5[:, :, :, 0, :]
                x2v = y5[:, :, :, 1, :]
                tt = [self.view(RXb + 36864 + i * 1024, 1024, F32, "p (h a f) -> p h a f", a=2, f=32)
                      for i in range(4)]
                rk = [("RX", "ropet", i) for i in range(4)]
                S.add("dve", lambda e, o=tt[0], a=x1v, c=cosb: e.tensor_tensor(o, a, c, ALU.mult),
                      reads=[("RX", "yn"), ("C", "rope")], writes=[rk[0]])
                S.add("dve", lambda e, o=tt[1], a=x2v, c=sinb: e.tensor_tensor(o, a, c, ALU.mult),
                      reads=[("RX", "yn"), ("C", "rope")], writes=[rk[1]])
                S.add("dve", lambda e, o=o5[:, :, :, 0, :], a=tt[0], c=tt[1]: e.tensor_tensor(o, a, c, ALU.subtract),
                      reads=[rk[0], rk[1]], writes=[ybk])
                S.add("dve", lambda e, o=tt[2], a=x2v, c=cosb: e.tensor_tensor(o, a, c, ALU.mult),
                      reads=[("RX", "yn"), ("C", "rope")], writes=[rk[2]])
                S.add("dve", lambda e, o=tt[3], a=x1v, c=sinb: e.tensor_tensor(o, a, c, ALU.mult),
                      reads=[("RX", "yn"), ("C", "rope")], writes=[rk[3]])
                S.add("dve", lambda e, o=o5[:, :, :, 1, :], a=tt[2], c=tt[3]: e.tensor_tensor(o, a, c, ALU.add),
                      reads=[rk[2], rk[3]], writes=[ybk])
        for s in subs:
            b = banks[s]
            bk = ("P", "bank", b)
            ybf_off = RXb + 53248 + s * 1024
            ybk = ("RX", "ybf", s)
            ybf2 = self.view(ybf_off, 1024, BF16)

            def trf(e, b=b, ybf2=ybf2):
                ins = None
                for h in range(4):
                    ins = e.transpose(self.bank_bf(b)[:, h * 128:(h + 1) * 128], ybf2[:, h * 128:(h + 1) * 128],
                                      self.identb)
                return ins
            S.add("pe", trf, reads=[ybk, ("C", "identb")], writes=[bk])
            dst = dest_fn(s)
            srcv = self.bank_bf(b)[:, 0:512].rearrange("p (h t) -> p h t", t=128)
            S.add("act", lambda e, dst=dst, srcv=srcv: e.copy(dst, srcv), reads=[bk], writes=[dest_key])

    def post_v(self, banks, tile, vcol0, subs=None):
        S = self.S
        for s in (range(NS) if subs is None else subs):
            b = banks[s]
            vst = self.view(self.RX + 51200 + (s % 2) * 1024, 1024, BF16)
            vk = ("RX", "vst", s % 2)
            S.add("act", lambda e, vst=vst, b=b: e.copy(vst, self.bank(b)), reads=[("P", "bank", b)], writes=[vk])
            r0 = tile * T + s * 128
            dst = self.v_d[r0:r0 + 128, vcol0:vcol0 + 512]
            S.add("sp", lambda e, dst=dst, vst=vst: e.dma_start(out=dst, in_=vst), reads=[vk],
                  writes=[("VD", vcol0 // 512)], dma=("vst", s % 2))

    CG_QA = [0, 1, 2, 3]
    CG_KA = 4
    CG_VA = 5
    CG_QB = [6, 7, 8]
    CG_KB = [9, 10, 11]
    CG_VB = [12, 13, 14]

    h1_off = None

    def h1T_view(self, off=None):
        off = self.h1_off if off is None else off
        return self.view(off, 32768, BF16, "p (k t) -> p k t", t=T)

    def h1_key(self, off=None):
        off = self.h1_off if off is None else off
        return ("RA" if off == self.RA else "RB", "h1T")

    def compute_h1T(self, tile, off=None, subs=None, banks=None, phase="both"):
        if off is None:
            off = self.RA
        srcs = [("dram", self.x_seq[tile * T + s * 128: tile * T + (s + 1) * 128, :]) for s in range(NS)]
        self.norm_T(srcs, self.g1T, self.h1T_view(off), self.h1_key(off), self.RX, "n1", subs=subs, banks=banks,
                    phase=phase)

    def kv_pass(self, tile, hooks=None):
        h1T = self.h1T_view()
        act = lambda kc: h1T[:, kc, :]
        akeys = [self.h1_key()]
        kcount = [0]

        def k_job(cg, head0, gain_idx, rope, subs=None):
            ki = kcount[0] % 2
            kcount[0] += 1
            kst = self.view(self.RX + 43008 + ki * 4096, 4096, BF16, "p (h t) -> p h t", t=T)
            kk = ("RX", "kst", ki)
            sl = list(range(NS)) if subs is None else list(subs)
            c_lo, c_hi = sl[0] * 128, (sl[-1] + 1) * 128

            def post(banks):
                self.post_qk(banks, tile, gain_idx, rope,
                             lambda s: kst[:, :, s * 128:(s + 1) * 128], kk, subs=sl)
                dst = self.kt_d[head0:head0 + 4, :, tile * T + c_lo:tile * T + c_hi].rearrange("h p t -> p h t")
                self.S.add("sp", lambda e: e.dma_start(out=dst, in_=kst[:, :, c_lo:c_hi]), reads=[kk],
                           writes=[("KD", head0 // 4)], dma=("kst", ki))
            self.job(lambda: self.stream_tok(self.w_in, cg * 512, 512, KC, act, akeys, subs=subs), post)

        def v_job(cg, vcol0, subs=None):
            self.job(lambda: self.stream_tok(self.w_in, cg * 512, 512, KC, act, akeys, subs=subs),
                     lambda banks: self.post_v(banks, tile, vcol0, subs=subs))

        gsubs = {0: None, 1: None, 2: None}
        if tile in self.B_SUBS:
            gsubs = {0: self.B_SUBS[tile][0], 1: self.B_SUBS[tile][1], 2: None}
        jobs = [lambda: k_job(self.CG_KA, 0, 1, True), lambda: v_job(self.CG_VA, 0)]
        for g in range(3):
            jobs.append(lambda g=g: k_job(self.CG_KB[g], 4 + 4 * g, 3, False, subs=gsubs[g]))
            jobs.append(lambda g=g: v_job(self.CG_VB[g], 512 * (g + 1), subs=gsubs[g]))
        for i, j in enumerate(jobs):
            j()
            if hooks and i in hooks:
                hooks[i]()

    B_SUBS = {2: {0: [0], 1: [0, 1]}, 3: {0: [3], 1: [2, 3]}}

    def free_set_banks(self):
        s_ = self.set_i % 2
        return [4 * s_ + j for j in range(4)]

    def q_pass(self, tile):
        h1T = self.h1T_view()
        act = lambda kc: h1T[:, kc, :]
        akeys = [self.h1_key()]
        QT = self.view(self.RB, 28672, BF16, "p (h t) -> p h t", t=T)
        allq = self.CG_QA + self.CG_QB
        for i in (0, 4, 1, 5, 2, 6, 3):
            cg = allq[i]
            head0 = 4 * i
            rope = i < 4
            gain_idx = 0 if rope else 2

            def post(banks, head0=head0, rope=rope, gain_idx=gain_idx):
                self.post_qk(banks, tile, gain_idx, rope,
                             lambda s: QT[:, head0:head0 + 4, s * 128:(s + 1) * 128], ("RB", "QT"))
            self.job(lambda cg=cg: self.stream_tok(self.w_in, cg * 512, 512, KC, act, akeys), post)

    def attention(self, tile):
        S = self.S
        self.flush()
        S.barrier("RX")
        S.barrier("RT")
        RX, RT = self.RX, self.RT
        QT = self.view(self.RB, 28672, BF16, "p (h t) -> p h t", t=T)
        oT = self.view(RX + 45056, 20480, BF16, "p (h t) -> p h t", t=T)
        kt = [self.view(RX + i * 4096, 4096, BF16) for i in range(2)]
        vv = [self.view(RX + 8192 + i * 4096, 4096, BF16, "p (c d) -> p c d", d=128) for i in range(2)]
        strips = [self.view(RX + 16384, 7680, F32), self.view(RX + 24064, 7680, F32)]
        accO = self.view(RX + 31744, 8192, F32, "p (s t) -> p s t", t=T)
        accD = self.view(RT, 8192, F32, "p (s t) -> p s t", t=T)
        NP_, NT_ = 4, 3
        P = [self.view(RT + 8192 + i * 1024, 1024, BF16) for i in range(NP_)]
        tmp = [self.view(RT + 12288 + i * 2048, 2048, F32) for i in range(NT_)]
        st = {"kv": 0, "p": 0, "t": 0, "s": 0, "od": 0}
        q0 = tile * T

        def load_kv(kvidx, vcol, chunks=None):
            sl = st["kv"] % 2
            st["kv"] += 1
            cs = list(range(16)) if chunks is None else sorted(chunks)
            runs = []
            for c in cs:
                if runs and runs[-1][1] == c - 1:
                    runs[-1][1] = c
                else:
                    runs.append([c, c])
            for (a, b_) in runs:
                S.add("sp", lambda e, a=a, b_=b_: e.dma_start(out=kt[sl][:, a * 128:(b_ + 1) * 128],
                                                            in_=self.kt_d[kvidx][:, a * 128:(b_ + 1) * 128]),
                      reads=[("KD", kvidx // 4)], writes=[("RX", "kt", sl)], dma=("kt", sl))
                src = self.v_d[a * 128:(b_ + 1) * 128, vcol:vcol + 128].rearrange("(c p) d -> p c d", p=128)
                S.add("sp", lambda e, a=a, b_=b_, src=src: e.dma_start(out=vv[sl][:, a:b_ + 1, :], in_=src),
                      reads=[("VD", vcol // 512)], writes=[("RX", "v", sl)], dma=("v", sl))
            return sl

        def load_strip(bh, w):
            g_ = bh // 4
            offs = [q0 - c * 128 + (C_OWN if w == 0 else C_OTH) for c in needed_chunks(g_) if (c >= 8) == (w == 1)]
            if not offs:
                return
            lo, hi = min(offs), max(offs) + T
            S.add("sp", lambda e: e.dma_start(out=strips[w][:, lo:hi], in_=self.c_strip[w, bh][:, lo:hi]),
                  writes=[("RX", "strip", w)], dma=("strip", w))

        items = []
        deferred = []

        def add_head(qidx, kvload, bh, epilogue, chunks):
            ctx = {"qidx": qidx, "kvload": kvload, "bh": bh, "epi": epilogue, "sl": None, "sb": {}}
            for i, c in enumerate(chunks):
                items.append((ctx, c, i == 0, i == len(chunks) - 1))

        _nc_cache = {}

        def needed_chunks(g):
            if g in _nc_cache:
                return _nc_cache[g]
            idx = _STRIP_IDX
            out = []
            for c in range(16):
                w = 0 if c < 8 else 1
                off = q0 - c * 128 + (C_OWN if w == 0 else C_OTH)
                need = False
                for hh in range(2):
                    if (idx[(w, hh, g)][:, off:off + T] != 32).any():
                        need = True
                if need:
                    out.append(c)
            if g in (0, 1):
                for c in out:
                    if c >= 8:
                        assert (c % 4) in self.B_SUBS[c // 4][g], (g, c)
            _nc_cache[g] = out
            return out

        for kv in range(4):
            for g in range(4):
                h = kv * 4 + g

                def epi(ob, db, h=h):
                    ti = st["t"] % NT_
                    st["t"] += 1
                    S.add("dve", lambda e: e.reciprocal(tmp[ti], self.bank(db)),
                          reads=[("P", "bank", db)], writes=[("RT", "tmp", ti)])
                    S.add("dve", lambda e: e.tensor_tensor(oT[:, h, :], self.bank(ob), tmp[ti], ALU.mult),
                          reads=[("P", "bank", ob), ("RT", "tmp", ti)], writes=[("RX", "oT")])
                add_head(h, (kv, kv * 128) if g == 0 else None, None, epi, list(range(16)))
        for g in range(3):
            chunks = needed_chunks(g)
            for s4 in range(4):
                bh = 4 * g + s4

                def epi(ob, db, g=g, s4=s4):
                    ak = ("RX", "acc", s4)
                    if g == 0:
                        S.add("dve", lambda e: e.tensor_copy(accO[:, s4, :], self.bank(ob)),
                              reads=[("P", "bank", ob)], writes=[ak])
                        S.add("dve", lambda e: e.tensor_copy(accD[:, s4, :], self.bank(db)),
                              reads=[("P", "bank", db)], writes=[("RT", "accD", s4)])
                    else:
                        S.add("dve", lambda e: e.tensor_tensor(accO[:, s4, :], self.bank(ob), accO[:, s4, :], ALU.add),
                              reads=[("P", "bank", ob), ak], writes=[ak])
                        S.add("dve", lambda e: e.tensor_tensor(accD[:, s4, :], self.bank(db), accD[:, s4, :], ALU.add),
                              reads=[("P", "bank", db), ("RT", "accD", s4)], writes=[("RT", "accD", s4)])
                    if g == 2:
                        def fin(s4=s4, ak=ak):
                            S.add("dve", lambda e: e.reciprocal(accD[:, s4, :], accD[:, s4, :]),
                                  reads=[("RT", "accD", s4)], writes=[("RT", "accD", s4)])
                            S.add("dve", lambda e: e.tensor_tensor(oT[:, 16 + s4, :], accO[:, s4, :], accD[:, s4, :], ALU.mult),
                                  reads=[ak, ("RT", "accD", s4)], writes=[("RX", "oT")])
                        deferred.append(fin)
                add_head(16 + bh, (4 + bh, 512 * (g + 1) + s4 * 128, tuple(chunks)), bh, epi, chunks)

        loads = []
        for (ctx, c, first, last) in items:
            if first:
                if ctx["kvload"] is not None:
                    loads.append(ctx["kvload"])
                ctx["li"] = len(loads) - 1
                ctx["sl"] = ctx["li"] % 2
        issued = [0]

        def ensure_loads(upto):
            while issued[0] <= min(upto, len(loads) - 1):
                load_kv(*loads[issued[0]])
                issued[0] += 1

        def issue_S(i):
            ctx, c, first, last = items[i]
            if first:
                ensure_loads(ctx["li"])
            sl = ctx["sl"]
            qidx = ctx["qidx"]
            sb = st["s"] % 4
            st["s"] += 1
            ctx["sb"][c] = sb
            S.add("pe", lambda e: e.matmul(self.bank(sb), kt[sl][:, c * 128:(c + 1) * 128], QT[:, qidx, :],
                                           start=True, stop=True),
                  reads=[("RX", "kt", sl), ("RB", "QT")], writes=[("P", "bank", sb)])

        LOOK = 3
        for i in range(min(LOOK, len(items))):
            issue_S(i)
        for i in range(len(items)):
            if i + LOOK < len(items):
                issue_S(i + LOOK)
            ctx, c, first, last = items[i]
            if first:
                pair = st["od"] % 2
                st["od"] += 1
                ctx["ob"], ctx["db"] = 4 + 2 * pair, 5 + 2 * pair
                if ctx["kvload"] is not None:
                    ensure_loads(ctx["li"] + 1)
                if ctx["bh"] is not None:
                    if not st.get("s0_loaded", False):
                        load_strip(ctx["bh"], 0)
                    st["s0_loaded"] = False
                    load_strip(ctx["bh"], 1)
            ob, db, sl, bh = ctx["ob"], ctx["db"], ctx["sl"], ctx["bh"]
            sb = ctx["sb"][c]
            pi = st["p"] % NP_
            st["p"] += 1
            if bh is None:
                S.add("act", lambda e, sb=sb, pi=pi: e.activation(P[pi], self.bank(sb), AF.Exp, scale=SCALE),
                      reads=[("P", "bank", sb)], writes=[("RT", "P", pi)])
            else:
                ti = st["t"] % NT_
                st["t"] += 1
                w = 0 if c < 8 else 1
                off = q0 - c * 128 + (C_OWN if w == 0 else C_OTH)
                assert 0 <= off and off + T <= SW_OWN
                sv = strips[w][:, off:off + T]
                S.add("dve", lambda e, sb=sb, ti=ti, sv=sv: e.scalar_tensor_tensor(
                    tmp[ti], self.bank(sb), SCALE, sv, ALU.mult, ALU.add),
                    reads=[("P", "bank", sb), ("RX", "strip", w)], writes=[("RT", "tmp", ti)])
                S.add("act", lambda e, ti=ti, pi=pi: e.activation(P[pi], tmp[ti], AF.Exp),
                      reads=[("RT", "tmp", ti)], writes=[("RT", "P", pi)])

            def pv(e, c=c, pi=pi, ob=ob, db=db, sl=sl, first=first, last=last):
                e.matmul(self.bank(ob), vv[sl][:, c, :], P[pi], start=first, stop=last)
                return e.matmul(self.bank(db), self.onesb, P[pi], start=first, stop=last)
            S.add("pe", pv, reads=[("RX", "v", sl), ("RT", "P", pi), ("C", "onesb")],
                  writes=[("P", "bank", ob), ("P", "bank", db)])
            if bh is not None and c < 8:
                rest = [items[j][1] for j in range(i + 1, len(items)) if items[j][0] is ctx]
                if not any(cc < 8 for cc in rest):
                    nxt = [items[j][0] for j in range(i + 1, len(items)) if items[j][0] is not ctx]
                    if nxt and nxt[0]["bh"] is not None:
                        load_strip(nxt[0]["bh"], 0)
                        st["s0_loaded"] = True
            if last:
                ctx["epi"](ob, db)
        for fin in deferred:
            fin()

    def merge(self, tile):
        S = self.S
        self.flush()
        S.barrier("RB")
        S.barrier("RX")
        S.barrier("RT")
        RX = self.RX
        h1T = self.h1T_view()
        oT = self.view(RX + 45056, 20480, BF16, "p (h t) -> p h t", t=T)
        mergedT = self.view(self.RB, 32768, BF16, "p (k t) -> p k t", t=T)
        sa = self.view(RX, 8192, F32, "p (j t) -> p j t", t=T)
        sb_ = self.view(RX + 8192, 8192, F32, "p (j t) -> p j t", t=T)
        mm = self.view(RX + 16384, 8192, F32, "p (j t) -> p j t", t=T)
        tt = self.view(RX + 24576, 2048, F32)
        hk = [("RA", "h1T")]
        ok = [("RX", "oT")]
        for cg in range(8):
            def post_ga(banks, cg=cg):
                for j in range(4):
                    bias = self.bgT[:, cg * 4 + j:cg * 4 + j + 1]
                    S.add("act", lambda e, j=j, b=banks[j], bias=bias: e.activation(sa[:, j, :], self.bank(b), AF.Sigmoid, bias=bias),
                          reads=[("P", "bank", banks[j]), ("C", "sm")], writes=[("RX", "sa")])
            self.job(lambda cg=cg: self.stream_feat(self.w_in, 0, 7680 + cg * 512, KC, lambda kc: h1T[:, kc, :], hk), post_ga)

            def post_pa(banks):
                for j in range(4):
                    S.add("dve", lambda e, j=j, b=banks[j]: e.tensor_tensor(mm[:, j, :], self.bank(b), sa[:, j, :], ALU.mult),
                          reads=[("P", "bank", banks[j]), ("RX", "sa")], writes=[("RX", "mm")])
            self.job(lambda cg=cg: self.stream_feat(self.w_pa, 0, cg * 512, 16, lambda kc: oT[:, kc, :], ok), post_pa)

            def post_gb(banks, cg=cg):
                for j in range(4):
                    bias = self.bgT[:, 32 + cg * 4 + j:32 + cg * 4 + j + 1]
                    S.add("act", lambda e, j=j, b=banks[j], bias=bias: e.activation(sb_[:, j, :], self.bank(b), AF.Sigmoid, bias=bias),
                          reads=[("P", "bank", banks[j]), ("C", "sm")], writes=[("RX", "sb")])
            self.job(lambda cg=cg: self.stream_feat(self.w_in, 0, 11776 + cg * 512, KC, lambda kc: h1T[:, kc, :], hk), post_gb)

            def post_pb(banks, cg=cg):
                for j in range(4):
                    S.add("dve", lambda e, j=j, b=banks[j]: e.tensor_tensor(tt, self.bank(b), sb_[:, j, :], ALU.mult),
                          reads=[("P", "bank", banks[j]), ("RX", "sb")], writes=[("RX", "tt")])
                    S.add("dve", lambda e, j=j: e.tensor_tensor(mergedT[:, cg * 4 + j, :], tt, mm[:, j, :], ALU.add),
                          reads=[("RX", "tt"), ("RX", "mm")], writes=[("RB", "mergedT")])
            self.job(lambda cg=cg: self.stream_feat(self.w_pb, 0, cg * 512, 4, lambda kc: oT[:, 16 + kc, :], ok), post_pb)

    def wout(self, tile):
        S = self.S
        self.flush()
        S.barrier("RA")
        S.barrier("RX")
        mergedT = self.view(self.RB, 32768, BF16, "p (k t) -> p k t", t=T)
        x1 = self.view(self.RX, 65536, F32, "p (s d) -> p s d", d=D)
        xres = [self.view(self.RA + i * 8192, 8192, F32, "p (s c) -> p s c", c=512) for i in range(2)]
        for cg in range(8):
            i = cg % 2
            src = self.x_seq[tile * T:(tile + 1) * T, cg * 512:(cg + 1) * 512].rearrange("(s p) c -> p s c", p=128)
            S.add("sp", lambda e, i=i, src=src: e.dma_start(out=xres[i], in_=src), writes=[("RA", "xres", i)], dma=("xres", i))

            def post(banks, cg=cg, i=i):
                for s in range(NS):
                    S.add("dve", lambda e, s=s, b=banks[s]: e.tensor_tensor(x1[:, s, cg * 512:(cg + 1) * 512], self.bank(b), xres[i][:, s, :], ALU.add),
                          reads=[("P", "bank", banks[s]), ("RA", "xres", i)], writes=[("RX", "x1", s)])
            self.job(lambda cg=cg: self.stream_tok(self.w_out, cg * 512, 512, KC, lambda kc: mergedT[:, kc, :], [("RB", "mergedT")]), post)

    def ffn(self, tile):
        S = self.S
        self.flush()
        S.barrier("RA")
        S.barrier("RB")
        S.barrier("RT")
        x1 = self.view(self.RX, 65536, F32, "p (s d) -> p s d", d=D)
        h2T = self.view(self.RB, 32768, BF16, "p (k t) -> p k t", t=T)
        srcs = [("sbuf", x1[:, s, :], [("RX", "x1", s)]) for s in range(NS)]
        self.norm_T(srcs, self.g2T, h2T, ("RB", "h2T"), self.RA, "n2")
        S.barrier("RA")
        sg = [self.view(self.RA + i * 8192, 8192, F32, "p (j t) -> p j t", t=T) for i in range(2)]
        aT = [self.view(self.RA + 16384 + i * 4096, 4096, BF16, "p (j t) -> p j t", t=T) for i in range(2)]
        hk = [("RB", "h2T")]
        gsz = [4] * 20 + [3, 3]
        gst = [sum(gsz[:i]) for i in range(len(gsz))]
        NG = len(gsz)
        assert sum(gsz) * 128 == DFF

        def njof(i):
            return gsz[i]

        def GU(i):
            nj = njof(i)
            p = i % 2

            def post_g(banks):
                for j in range(nj):
                    S.add("act", lambda e, j=j, b=banks[j]: e.activation(sg[p][:, j, :], self.bank(b), AF.Silu),
                          reads=[("P", "bank", banks[j])], writes=[("RA", "sg", p)])
            self.job(lambda: self.stream_feat(self.w_g, 0, gst[i] * 128, KC, lambda kc: h2T[:, kc, :], hk, nj=nj), post_g)

            def post_u(banks):
                for j in range(nj):
                    S.add("dve", lambda e, j=j, b=banks[j]: e.tensor_tensor(aT[p][:, j, :], self.bank(b), sg[p][:, j, :], ALU.mult),
                          reads=[("P", "bank", banks[j]), ("RA", "sg", p)], writes=[("RA", "aT", p)])
            self.job(lambda: self.stream_feat(self.w_u, 0, gst[i] * 128, KC, lambda kc: h2T[:, kc, :], hk, nj=nj), post_u)

        def Dn(i):
            nj = njof(i)
            p = i % 2
            for cg in range(8):
                def post(banks, cg=cg):
                    for s in range(NS):
                        xs_ = x1[:, s, cg * 512:(cg + 1) * 512]
                        S.add("dve", lambda e, b=banks[s], xs_=xs_: e.tensor_tensor(xs_, self.bank(b), xs_, ALU.add),
                              reads=[("P", "bank", banks[s]), ("RX", "x1", s)], writes=[("RX", "x1", s)])
                    if i == NG - 1:
                        dst = self.out[tile * T:(tile + 1) * T, cg * 512:(cg + 1) * 512].rearrange("(s p) c -> p s c", p=128)
                        o = S.add("sp", lambda e: e.dma_start(out=dst, in_=x1[:, :, cg * 512:(cg + 1) * 512]),
                                  reads=[("RX", "x1", s_) for s_ in range(NS)], dma=("out", cg % 4))
                        self.final.append(o)
                self.job(lambda cg=cg: self.stream_tok(self.w_d, cg * 512, 512, nj, lambda kc: aT[p][:, kc, :],
                                                       [("RA", "aT", p)], krow0=gst[i]), post)
        GU(0)
        for i in range(NG):
            if i + 1 < NG:
                GU(i + 1)
            Dn(i)
        self.flush()

    def build(self):
        self.setup()
        self.load_consts()
        S = self.S
        order = (2, 3, 1, 0)
        bufs = {2: self.RB, 3: self.RA, 1: self.RB, 0: self.RA}
        self.compute_h1T(order[0], off=bufs[order[0]])
        for i, tile in enumerate(order):
            self.h1_off = bufs[tile]
            hooks = None
            if i + 1 < len(order):
                nt = order[i + 1]
                def h3(nt=nt):
                    self.compute_h1T(nt, off=bufs[nt], subs=(0, 1), banks=self.free_set_banks(), phase="back")
                    self.compute_h1T(nt, off=bufs[nt], subs=(2, 3), phase="front")
                hooks = {
                    1: lambda nt=nt: self.compute_h1T(nt, off=bufs[nt], subs=(0, 1), phase="front"),
                    3: h3,
                    5: lambda nt=nt: self.compute_h1T(nt, off=bufs[nt], subs=(2, 3), banks=self.free_set_banks(),
                                                      phase="back"),
                }
            self.kv_pass(tile, hooks)
        self.h1_off = self.RA
        S.barrier("RB")
        for tile in (0, 1):
            if tile == 1:
                self.flush()
                for r in ("RA", "RB", "RX", "RT"):
                    S.barrier(r)
                self.compute_h1T(1)
            self.q_pass(tile)
            self.attention(tile)
            self.merge(tile)
            self.wout(tile)
            self.ffn(tile)
        S.emit(self.nc, self.es, self.final)
        return self.nc


def xs_off_region(self, off):
    return "RX" if off >= self.RX and off < self.RT else ("RA" if off < self.RB else "RB")


def _t5_bucket_np(rel):
    nb = 16
    max_exact = 8
    rel = np.asarray(rel, dtype=np.int64)
    side = np.where(rel > 0, nb, 0)
    n = np.abs(rel)
    nf = np.maximum(n, 1).astype(np.float32)
    large = max_exact + (np.log(nf / np.float32(max_exact)) / np.float32(math.log(1024 / max_exact))
                         * np.float32(nb - max_exact)).astype(np.int32)
    large = np.minimum(large, nb - 1)
    return side + np.where(n < max_exact, n, large)


_B_PATTERNS = ((128, 1), (512, 4), (2048, 16))


def _strip_index_tables():
    out = {}
    kp = np.arange(128)[:, None]
    m = np.arange(SW_OWN)[None, :]
    for hh in range(2):
        for which in range(2):
            if which == 0:
                delta = kp - m + C_OWN
            else:
                delta = kp - m + C_OTH - hh * 2048
            for g, (window, dil) in enumerate(_B_PATTERNS):
                ok = (delta % dil == 0) & (np.abs(delta) <= (window // (2 * dil)) * dil)
                idx = np.where(ok, _t5_bucket_np(delta), 32)
                out[(which, hh, g)] = idx
    return out


_STRIP_IDX = _strip_index_tables()


def _rope_table(hh):
    pos = np.arange(SEQ)
    actual = np.where(pos < 1024, hh * 1024 + pos, (1 - hh) * 1024 + (pos - 1024))
    row = (actual // 64).astype(np.float64)
    col = (actual % 64).astype(np.float64)
    inv = 10000.0 ** (-np.arange(0, 64, 2, dtype=np.float64) / 64.0)
    ar = row[:, None] * inv[None, :]
    ac = col[:, None] * inv[None, :]
    tab = np.stack([np.cos(ar), np.cos(ac), np.sin(ar), np.sin(ac)], axis=1).astype(np.float32)
    tab = tab.reshape(16, 128, 4, 32).transpose(1, 0, 2, 3).reshape(128, 16 * 4 * 32)
    return np.ascontiguousarray(tab)


def prep_core_inputs(inputs, core):
    b, hh = core // 2, core % 2
    f32 = np.float32
    x = np.asarray(inputs["x"], dtype=f32)
    own = x[b, hh * 1024:(hh + 1) * 1024]
    oth = x[b, (1 - hh) * 1024:(2 - hh) * 1024]
    x_seq = np.ascontiguousarray(np.concatenate([own, oth], axis=0))
    cs = np.zeros((128, 1024), dtype=f32)
    cs[:, 0:128] = np.eye(128, dtype=f32)
    cs[:, 128:160] = np.asarray(inputs["norm1_g"], f32)[0].reshape(32, 128).T
    cs[:, 160:192] = np.asarray(inputs["norm2_g"], f32)[0].reshape(32, 128).T
    bg = np.asarray(inputs["b_gate"], f32)[0]
    cs[:, 192:224] = bg[0].reshape(32, 128).T
    cs[:, 224:256] = bg[1].reshape(32, 128).T
    for i, nm in enumerate(["q_norm_a", "k_norm_a", "q_norm_b", "k_norm_b"]):
        cs[:, 256 + i * 128:256 + (i + 1) * 128] = np.asarray(inputs[nm], f32)[0][None, :]
    rb = np.asarray(inputs["rel_bias"], f32)
    rb_ext = np.concatenate([rb, np.full((1, rb.shape[1]), NEG, dtype=f32)], axis=0)
    idx = _strip_index_tables()
    strips = np.empty((2, 12, 128, SW_OWN), dtype=f32)
    for which in range(2):
        for g in range(3):
            ii = idx[(which, hh, g)]
            for s in range(4):
                strips[which, 4 * g + s] = rb_ext[ii, 4 * g + s]
    m = {
        "x_seq": x_seq,
        "w_in": np.asarray(inputs["w_in"], f32)[0],
        "w_proj_a": np.asarray(inputs["w_proj_a"], f32)[0],
        "w_proj_b": np.asarray(inputs["w_proj_b"], f32)[0],
        "w_out": np.asarray(inputs["w_out"], f32)[0],
        "w_ffn_gate": np.asarray(inputs["w_ffn_gate"], f32)[0],
        "w_ffn_up": np.asarray(inputs["w_ffn_up"], f32)[0],
        "w_ffn_down": np.asarray(inputs["w_ffn_down"], f32)[0],
        "c_small": cs,
        "c_rope": _rope_table(hh),
        "c_strip": strips,
    }
    return m


_CACHE = {}


def kernel(**inputs):
    if "nc" not in _CACHE:
        _CACHE["nc"] = Builder().build()
    nc = _CACHE["nc"]
    in_maps = [prep_core_inputs(inputs, c) for c in range(8)]
    res = run_bass_kernel_spmd(nc, in_maps, core_ids=list(range(8)))
    out = np.empty((BATCH, SEQ, D), dtype=np.float32)
    for c in range(8):
        b, hh = c // 2, c % 2
        out[b, hh * 1024:(hh + 1) * 1024] = np.asarray(res.results[c]["out"], dtype=np.float32)
    return out
```
